# Optimizing a Trainium2 kernel written in Bass

```python
import jax, jax.numpy as jnp
from jax import lax
import numpy as np

D_MODEL = 1024
BATCH = 32
SEQ = 256
DEPTH = 2
DEC_BATCH = 4
DEC_SEQ = 1024
PAST_LEN = 256

GRID_W = 64
N_HEADS = 16
HEAD_DIM = D_MODEL // N_HEADS
WIN_H = 8
WIN_W = 16
Q_BLK_W = 16
K_BLK_W = 32
D_FF = 2816
N_MIXERS = 2
N_SC = (DEPTH + 1) // 2
N_NA = DEPTH // 2
CTX_Q_BLK = 128
EPS = 1e-6
NEG_INF = -1e30

kernel_name = "hybrid_shortconv_natten_diffusion_step"


def rmsnorm(x, g):
    xf = x.astype(jnp.float32)
    y = xf * lax.rsqrt(jnp.mean(xf * xf, axis=-1, keepdims=True) + EPS)
    return y.astype(x.dtype) * g


def adaln_params(cvec, w, b):
    m = jax.nn.silu(cvec) @ w + b
    return jnp.split(m[:, None, :], 6, axis=-1)


def modulate(h, shift, scale):
    return h * (1.0 + scale) + shift


def dwconv3(x, w, b):
    xp = jnp.pad(x, ((0, 0), (1, 1), (0, 0)))
    return xp[:, :-2] * w[0] + xp[:, 1:-1] * w[1] + xp[:, 2:] * w[2] + b


def short_conv_mixer(h, w_in, conv_w, conv_b, w_out):
    bg, cg, xv = jnp.split(h @ w_in, 3, axis=-1)
    return (bg * dwconv3(cg * xv, conv_w, conv_b)) @ w_out


def conv_ffn(h, w_up, conv_w, conv_b, w_down):
    u, g = jnp.split(h @ w_up, 2, axis=-1)
    u = dwconv3(u, conv_w, conv_b)
    return (jax.nn.gelu(u) * g) @ w_down


def split_heads(qkv):
    B, T, _ = qkv.shape
    qkv = qkv.reshape(B, T, 3, N_HEADS, HEAD_DIM).transpose(2, 0, 3, 1, 4)
    return qkv[0], qkv[1], qkv[2]


def merge_heads(o):
    B, H, T, dh = o.shape
    return o.transpose(0, 2, 1, 3).reshape(B, T, H * dh)


def context_attention(q, k, v):
    B, H, L, dh = q.shape
    nb = L // CTX_Q_BLK
    scale = HEAD_DIM ** -0.5
    qb = q.reshape(B, H, nb, CTX_Q_BLK, dh).transpose(2, 0, 1, 3, 4)

    def one_block(qblk):
        s = jnp.einsum('bhqd,bhkd->bhqk', qblk, k).astype(jnp.float32) * scale
        p = jax.nn.softmax(s, axis=-1).astype(v.dtype)
        return jnp.einsum('bhqk,bhkd->bhqd', p, v)

    o = lax.map(one_block, qb)
    return o.transpose(1, 2, 0, 3, 4).reshape(B, H, L, dh)


def neighbourhood_attention(q, k, v, k_ctx, v_ctx, rpb):
    B, H, N, dh = q.shape
    rows = N // GRID_W
    kh = min(WIN_H, rows)
    ncb = GRID_W // Q_BLK_W
    nk = kh * K_BLK_W
    scale = HEAD_DIM ** -0.5
    r = jnp.arange(rows)
    row_idx = jnp.clip(r - kh // 2, 0, rows - kh)[:, None] + jnp.arange(kh)
    j = jnp.arange(ncb)
    kb_start = jnp.clip(j * Q_BLK_W - WIN_W // 2, 0, GRID_W - K_BLK_W)
    col_idx = kb_start[:, None] + jnp.arange(K_BLK_W)
    qcol = j[:, None] * Q_BLK_W + jnp.arange(Q_BLK_W)
    col_start = jnp.clip(qcol - WIN_W // 2, 0, GRID_W - WIN_W)
    kc = col_idx[:, None, :]
    col_mask = (kc >= col_start[..., None]) & (kc < col_start[..., None] + WIN_W)
    dr = row_idx - r[:, None]
    dc = jnp.clip(kc - qcol[..., None], -(WIN_W - 1), WIN_W - 1)
    bias = rpb.astype(jnp.float32)[:, dr[:, None, None, :, None] + (WIN_H - 1),
                                   dc[None, :, :, None, :] + (WIN_W - 1)]
    bias = jnp.where(col_mask[None, None, :, :, None, :], bias, NEG_INF)
    bias = jnp.moveaxis(bias.reshape(H, rows, ncb, Q_BLK_W, nk), 1, 0)
    kg = k.reshape(B, H, rows, GRID_W, dh)
    vg = v.reshape(B, H, rows, GRID_W, dh)
    ri = row_idx[:, None, :, None]
    ci = col_idx[None, :, None, :]
    k_blk = jnp.moveaxis(kg[:, :, ri, ci].reshape(B, H, rows, ncb, nk, dh), 2, 0)
    v_blk = jnp.moveaxis(vg[:, :, ri, ci].reshape(B, H, rows, ncb, nk, dh), 2, 0)
    q_blk = jnp.moveaxis(q.reshape(B, H, rows, ncb, Q_BLK_W, dh), 2, 0)

    def one_row(args):
        q_r, k_r, v_r, b_r = args
        s_loc = jnp.einsum('bhjqd,bhjnd->bhjqn', q_r, k_r).astype(jnp.float32) * scale + b_r[None]
        s_ctx = jnp.einsum('bhjqd,bhld->bhjql', q_r, k_ctx).astype(jnp.float32) * scale
        p = jax.nn.softmax(jnp.concatenate([s_loc, s_ctx], axis=-1), axis=-1).astype(v_r.dtype)
        return (jnp.einsum('bhjqn,bhjnd->bhjqd', p[..., :nk], v_r)
                + jnp.einsum('bhjql,bhld->bhjqd', p[..., nk:], v_ctx))

    o = lax.map(one_row, (q_blk, k_blk, v_blk, bias))
    return jnp.moveaxis(o, 0, 2).reshape(B, H, N, dh)


def setup_inputs(seed: int = 0) -> dict:
    key = jax.random.key(seed)
    ks = jax.random.split(key, 24)
    D = D_MODEL
    nrm = jax.random.normal
    f32 = jnp.float32
    return {
        "x_prompt": nrm(ks[0], (BATCH, SEQ, D), f32),
        "x_sample": nrm(ks[1], (DEC_BATCH, DEC_SEQ, D), f32),
        "cache_k_ctx": nrm(ks[2], (DEC_BATCH, N_NA, N_HEADS, PAST_LEN, HEAD_DIM), f32),
        "cache_v_ctx": nrm(ks[3], (DEC_BATCH, N_NA, N_HEADS, PAST_LEN, HEAD_DIM), f32),
        "c": nrm(ks[4], (DEC_BATCH, D), f32),
        "c_ctx": nrm(ks[5], (D,), f32),
        "ada_w": nrm(ks[6], (DEPTH, D, 6 * D), f32) * (0.5 * D ** -0.5),
        "ada_b": nrm(ks[7], (DEPTH, 6 * D), f32) * 0.02,
        "norm_mix_g": 1.0 + 0.05 * nrm(ks[8], (DEPTH, D), f32),
        "norm_ffn_g": 1.0 + 0.05 * nrm(ks[9], (DEPTH, D), f32),
        "sc_w_in": nrm(ks[10], (N_SC, D, 3 * D), f32) * D ** -0.5,
        "sc_conv_w": nrm(ks[11], (N_SC, 3, D), f32) * 0.5,
        "sc_conv_b": nrm(ks[12], (N_SC, D), f32) * 0.02,
        "sc_w_out": nrm(ks[13], (N_SC, D, D), f32) * D ** -0.5,
        "na_w_qkv": nrm(ks[14], (N_NA, D, 3 * D), f32) * D ** -0.5,
        "na_rpb": nrm(ks[15], (N_NA, N_HEADS, 2 * WIN_H - 1, 2 * WIN_W - 1), f32) * 0.5,
        "na_w_o": nrm(ks[16], (N_NA, D, D), f32) * D ** -0.5,
        "ffn_w_up": nrm(ks[17], (DEPTH, D, 2 * D_FF), f32) * D ** -0.5,
        "ffn_conv_w": nrm(ks[18], (DEPTH, 3, D_FF), f32) * 0.5,
        "ffn_conv_b": nrm(ks[19], (DEPTH, D_FF), f32) * 0.02,
        "ffn_w_down": nrm(ks[20], (DEPTH, D_FF, D), f32) * D_FF ** -0.5,
        "final_g": 1.0 + 0.05 * nrm(ks[21], (D,), f32),
    }


def reference(x_prompt, x_sample, cache_k_ctx, cache_v_ctx, c, c_ctx, ada_w, ada_b,
              norm_mix_g, norm_ffn_g, sc_w_in, sc_conv_w, sc_conv_b, sc_w_out,
              na_w_qkv, na_rpb, na_w_o, ffn_w_up, ffn_conv_w, ffn_conv_b, ffn_w_down, final_g):
    xp = x_prompt
    xs = x_sample
    new_k, new_v = [], []
    for i in range(DEPTH):
        sh1p, sc1p, g1p, sh2p, sc2p, g2p = adaln_params(c_ctx[None, :], ada_w[i], ada_b[i])
        sh1s, sc1s, g1s, sh2s, sc2s, g2s = adaln_params(c, ada_w[i], ada_b[i])
        hp = modulate(rmsnorm(xp, norm_mix_g[i]), sh1p, sc1p)
        hs = modulate(rmsnorm(xs, norm_mix_g[i]), sh1s, sc1s)
        if i % N_MIXERS == 0:
            a = i // N_MIXERS
            yp = short_conv_mixer(hp, sc_w_in[a], sc_conv_w[a], sc_conv_b[a], sc_w_out[a])
            ys = short_conv_mixer(hs, sc_w_in[a], sc_conv_w[a], sc_conv_b[a], sc_w_out[a])
        else:
            b = i // N_MIXERS
            qp, kp, vp = split_heads(hp @ na_w_qkv[b])
            new_k.append(kp)
            new_v.append(vp)
            yp = merge_heads(context_attention(qp, kp, vp)) @ na_w_o[b]
            qs, ksl, vsl = split_heads(hs @ na_w_qkv[b])
            os_ = neighbourhood_attention(qs, ksl, vsl, cache_k_ctx[:, b], cache_v_ctx[:, b], na_rpb[b])
            ys = merge_heads(os_) @ na_w_o[b]
        xp = xp + g1p * yp
        xs = xs + g1s * ys
        hp = modulate(rmsnorm(xp, norm_ffn_g[i]), sh2p, sc2p)
        hs = modulate(rmsnorm(xs, norm_ffn_g[i]), sh2s, sc2s)
        xp = xp + g2p * conv_ffn(hp, ffn_w_up[i], ffn_conv_w[i], ffn_conv_b[i], ffn_w_down[i])
        xs = xs + g2s * conv_ffn(hs, ffn_w_up[i], ffn_conv_w[i], ffn_conv_b[i], ffn_w_down[i])
    y_prompt = rmsnorm(xp, final_g)
    y_sample = rmsnorm(xs, final_g)
    state_k_ctx = jnp.stack(new_k, axis=1)
    state_v_ctx = jnp.stack(new_v, axis=1)
    return (y_prompt, y_sample, state_k_ctx, state_v_ctx)
```

```python
import numpy as np
from contextlib import ExitStack
import concourse.bass as bass
import concourse.mybir as mybir
from concourse.bass_utils import run_bass_kernel_spmd

F32 = mybir.dt.float32
BF16 = mybir.dt.bfloat16
AF = mybir.ActivationFunctionType
ALU = mybir.AluOpType

D = 1024
DFF = 2816
NF = 22
H = 16
NEG = -30000.0
DEBUG = None
ATT_MODE = 9

COMPUTE = ("pe", "act", "dve", "pool")
QUEUES = ("sp", "actq", "poolq")
STREAM = {"pe": "tensor", "act": "scalar", "dve": "vector", "pool": "gpsimd",
          "sp": "sync", "actq": "scalar", "poolq": "gpsimd"}


class Buf:
    __slots__ = ("name", "last_w", "readers", "wsem", "wcount", "rsem", "rcount", "gen")

    def __init__(self, name):
        self.name = name
        self.gen = 0
        self.last_w = None
        self.readers = []
        self.wsem = None
        self.wcount = 0
        self.rsem = None
        self.rcount = 0


class Op:
    __slots__ = ("eng", "fn", "deps", "signal", "sig_idx", "is_dma", "dma_sem", "dma_target", "stream")

    def __init__(self, eng, fn):
        self.eng = eng
        self.fn = fn
        self.deps = []
        self.signal = False
        self.sig_idx = None
        self.is_dma = eng in QUEUES
        self.dma_sem = None
        self.dma_target = None
        self.stream = STREAM[eng]


class Prog:
    def __init__(self, nc, stack):
        self.nc = nc
        self.stack = stack
        self.ops = []
        self.final_waits = []
        self.nsem = 0

    def _sem(self, name):
        self.nsem += 1
        return self.stack.enter_context(self.nc.semaphore(name))

    def op(self, eng, fn, reads=(), writes=(), out_dma=False):
        o = Op(eng, fn)
        rr = []
        for b in reads:
            if isinstance(b, tuple):
                assert b[0].gen == b[1], f"stale ring/pt slot {b[0].name}: gen {b[0].gen} != {b[1]}"
                b = b[0]
            rr.append(b)
        reads = rr
        deps = []
        for b in reads:
            if b.last_w is not None:
                deps.append(b.last_w)
        for b in writes:
            if b.last_w is not None:
                deps.append(b.last_w)
            deps.extend(b.readers)
        seen = set()
        for d in deps:
            if id(d) in seen:
                continue
            seen.add(id(d))
            if d.eng == "pe" and eng == "pe":
                continue
            if not d.is_dma:
                d.signal = True
            o.deps.append(d)
        if o.is_dma:
            if out_dma:
                b = reads[0]
                if b.rsem is None:
                    b.rsem = self._sem("r_" + b.name)
                b.rcount += 16
                o.dma_sem, o.dma_target = b.rsem, b.rcount
                self.final_waits.append(o)
            else:
                b = writes[0]
                if b.wsem is None:
                    b.wsem = self._sem("w_" + b.name)
                b.wcount += 16
                o.dma_sem, o.dma_target = b.wsem, b.wcount
        for b in writes:
            b.last_w = o
            b.readers = []
        for b in reads:
            b.readers.append(o)
        self.ops.append(o)
        return o

    def emit(self):
        nc = self.nc
        esem = {e: self._sem("e_" + e) for e in COMPUTE}
        cnt = {e: 0 for e in COMPUTE}
        for o in self.ops:
            if not o.is_dma and o.signal:
                cnt[o.eng] += 1
                o.sig_idx = cnt[o.eng]
        streams = {"tensor": [], "scalar": [], "vector": [], "gpsimd": [], "sync": []}
        for o in self.ops:
            streams[o.stream].append(o)
        final = list(self.final_waits)

        def run(engine, ops, is_sync=False):
            waited = {}

            def wait(sem, val):
                k = id(sem)
                if waited.get(k, 0) >= val:
                    return
                waited[k] = val
                engine.wait_ge(sem, val)

            for o in ops:
                need = {}
                for d in o.deps:
                    sem, val = (d.dma_sem, d.dma_target) if d.is_dma else (esem[d.eng], d.sig_idx)
                    k = id(sem)
                    if k not in need or need[k][1] < val:
                        need[k] = (sem, val)
                for sem, val in need.values():
                    wait(sem, val)
                ins = o.fn(engine)
                if o.is_dma:
                    ins.then_inc(o.dma_sem, 16)
                elif o.signal:
                    ins.then_inc(esem[o.eng], 1)
            if is_sync:
                last = {}
                for o in final:
                    k = id(o.dma_sem)
                    if k not in last or last[k][1] < o.dma_target:
                        last[k] = (o.dma_sem, o.dma_target)
                for sem, val in last.values():
                    engine.wait_ge(sem, val)

        with nc.Block() as block:
            @block.tensor
            def _(e):
                run(e, streams["tensor"])

            @block.scalar
            def _(e):
                run(e, streams["scalar"])

            @block.vector
            def _(e):
                run(e, streams["vector"])

            @block.gpsimd
            def _(e):
                run(e, streams["gpsimd"])

            @block.sync
            def _(e):
                run(e, streams["sync"], is_sync=True)


def _pv_layout():
    off = {}
    n = 0

    def add(name, cnt):
        nonlocal n
        off[name] = n
        n += cnt
    for l in range(2):
        add(f"ada_b{l}", 48)
        add(f"gmix{l}", 8)
        add(f"gffn{l}", 8)
        for v in range(2):
            add(f"fcw{l}_{v}", 3 * NF)
        add(f"fcb{l}", NF)
    for v in range(2):
        add(f"scw_{v}", 24)
    add("scb", 8)
    add("gfin", 8)
    return off, n


PV_OFF, PV_N = _pv_layout()


class Tile:
    def __init__(self, col0, subs, segs, v):
        self.col0 = col0
        self.subs = subs
        self.segs = segs
        self.v = v
        self.n = sum(s[1] for s in subs)


def build_nc():
    nc = bass.Bass("TRN2", target_bir_lowering=False)

    def din(name, shape):
        return nc.dram_tensor(name, list(shape), F32, kind="ExternalInput").ap()

    def dout(name, shape):
        return nc.dram_tensor(name, list(shape), F32, kind="ExternalOutput").ap()

    xp_d = din("xp", [1024, D])
    xs_d = din("xs", [832, D])
    kc_d = din("kc", [H, 256, 64])
    vc_d = din("vc", [H, 256, 64])
    cvec_d = din("cvec", [128, 16])
    pvec_d = din("pvec", [128, PV_N])
    bias0_d = din("bias0", [H, 128, 3072])
    bias1_d = din("bias1", [128, H * 5])
    ada_d = din("ada_w", [2, D, 6 * D])
    win_d = din("sc_w_in", [D, 3 * D])
    wout_d = din("sc_w_out", [D, D])
    wqkv_d = din("na_w_qkv", [D, 3 * D])
    wo_d = din("na_w_o", [D, D])
    wup_d = din("ffn_w_up", [2, D, 2 * DFF])
    wdn_d = din("ffn_w_down", [2, DFF, D])
    yp_d = dout("yp", [1024, D])
    ys_d = dout("ys", [512, D])
    sk_d = dout("sk", [4, H, 256, 64])
    sv_d = dout("sv", [4, H, 256, 64])
    dbg_d = dout("dbg", [128, 8, 1024]) if DEBUG else None

    with ExitStack() as st:
        P = Prog(nc, st)

        def sb(name, shape, dt):
            return st.enter_context(nc.sbuf_tensor(name, list(shape), dt))

        def psum(name, shape, dt):
            return st.enter_context(nc.psum_tensor(name, list(shape), dt))

        R = 10
        NPT = 12
        xT = sb("xT", [128, 8, 1040], F32)
        actT = sb("actT", [128, 8, 1040], BF16)
        tmpT = sb("tmpT", [128, 8, 1040], BF16)
        ring = sb("ring", [128, R, 3072], BF16)
        E = sb("E", [128, 6, 832], F32)
        rstd = sb("rstd", [128, 832], F32)
        sq = sb("sq", [128, 2, 832], BF16)
        pv = sb("pv", [128, PV_N], F32)
        cv = sb("cv", [128, 16], F32)
        scT = sb("scT", [128, 16], BF16)
        mod = sb("mod", [128, 2, 2, 48], F32)
        stage = sb("stage", [128, 2, 1024], F32)
        identf = sb("identf", [128, 128], F32)
        identb = sb("identb", [128, 128], BF16)
        onesb = sb("onesb", [128, 128], BF16)
        epst = sb("epst", [128, 1], F32)
        QTz = sb("QTz", [128, 2, 2, 528], BF16)
        KT = sb("KT", [128, 2, 832], BF16)
        KcT = sb("KcT", [128, 8, 256], BF16)
        VcA = sb("VcA", [128, 2, H, 66], BF16)
        Vaug = sb("Vaug", [128, 2, 7, 2, 66], BF16)
        PT = sb("PT", [128, NPT, 512], BF16)
        PT1 = sb("PT1", [128, 2, 7, 16], BF16)
        Tt = sb("Tt", [128, 2, 512], F32)
        t80 = sb("t80", [128, 2, 80], F32)
        b1 = sb("b1", [128, H * 5], F32)
        Otok = sb("Otok", [128, 2, 5, 128], BF16)
        rc = sb("rc", [128, 2, 8], F32)
        abuf = sb("abuf", [128, 2, 1024], BF16)

        NB = 6
        banks = [psum(f"bank{i}", [128, 512], F32) for i in range(NB)]
        bankbf = [psum(f"bankbf{i}", [128, 1024], BF16) for i in range(1)]
        adabank = psum("adabank", [128, 512], F32)
        B_adabank = Buf("adabank")
        B_bank = [Buf(f"bank{i}") for i in range(NB)]
        B_bf = [Buf(f"bf{i}") for i in range(1)]
        state = {"bank": 0, "bf": 0, "ring": 0, "pt": 0, "half": 0, "tt": 0, "nb": NB}
        banks.append(adabank)
        B_bank.append(B_adabank)

        def next_bank():
            i = state["bank"]
            state["bank"] = (i + 1) % state["nb"]
            return banks[i], B_bank[i]

        def next_bf():
            i = state["bf"]
            state["bf"] = 0
            return bankbf[0], B_bf[0]

        B_ring = [[Buf(f"ring{i}_{j}") for j in range(3)] for i in range(R)]

        def next_slot():
            i = state["ring"]
            state["ring"] = (i + 1) % R
            return i

        B_x = [Buf(f"x{c}") for c in range(8)]
        B_X = [[Buf(f"X{t}_{c}") for c in range(8)] for t in range(2)]
        B_A = [[Buf(f"A{t}_{c}") for c in range(8)] for t in range(2)]
        B_HH = [[Buf(f"H{t}_{c}") for c in range(8)] for t in range(2)]
        B_E = [Buf(f"E{i}") for i in range(6)]
        B_rstd = Buf("rstd")
        B_sq = [Buf("sq0"), Buf("sq1")]
        B_pv = Buf("pv")
        B_cv = Buf("cv")
        B_scT = Buf("scT")
        B_modg = [[Buf(f"mod{l}_{g}") for g in range(6)] for l in range(2)]
        B_stage = [Buf("stage0"), Buf("stage1")]
        B_kv = [[Buf("kst0"), Buf("kst1")], [Buf("vst0"), Buf("vst1")]]
        B_const = Buf("const")
        B_QT = [Buf("QT0"), Buf("QT1")]
        B_KT = [Buf("KT0"), Buf("KT1")]
        B_KcT = Buf("KcT")
        B_VcA = Buf("VcA")
        B_Vaug = [Buf("Vaug0"), Buf("Vaug1")]
        B_PT = [Buf(f"PT{i}") for i in range(NPT)]
        B_PT1 = [Buf("PT1a"), Buf("PT1b")]
        B_T = [Buf("T0"), Buf("T1")]
        B_t80 = [Buf("t80a"), Buf("t80b")]
        B_b1 = Buf("b1")
        B_Otok = [Buf("Otok0"), Buf("Otok1")]
        B_rc = [Buf("rc0"), Buf("rc1")]
        B_ab = [Buf("ab0"), Buf("ab1")]

        P.op("pool", lambda e: e.memset(identf[:], 1.0), writes=[B_const])
        P.op("pool", lambda e: e.affine_select(out=identf[:], in_=identf[:], pattern=[[-1, 128]],
                                               compare_op=ALU.is_equal, fill=0.0, base=0, channel_multiplier=1),
             reads=[B_const], writes=[B_const])
        P.op("dve", lambda e: e.tensor_copy(out=identb[:], in_=identf[:]), reads=[B_const], writes=[B_const])
        P.op("dve", lambda e: e.memset(onesb[:], 1.0), writes=[B_const])
        P.op("dve", lambda e: e.memset(epst[:], 1e-6), writes=[B_const])
        P.op("dve", lambda e: e.memset(Vaug[:], 1.0), writes=B_Vaug)
        P.op("dve", lambda e: e.memset(QTz[:], 0.0), writes=B_QT)
        P.op("dve", lambda e: e.memset(VcA[:], 1.0), writes=[B_VcA])
        P.op("sp", lambda e: e.dma_start(out=pv[:], in_=pvec_d), writes=[B_pv])
        P.op("sp", lambda e: e.dma_start(out=cv[:], in_=cvec_d), writes=[B_cv])
        P.op("sp", lambda e: e.dma_start(out=b1[:], in_=bias1_d), writes=[B_b1])

        def pvc(name, i=0):
            o = PV_OFF[name] + i
            return pv[:, o:o + 1]

        def ring_ap(s, a, b):
            return ring[:, s, a:b]

        def load_full(dram_ap_128xN):
            s = next_slot()
            n = dram_ap_128xN.shape[-1] if len(dram_ap_128xN.shape) == 2 else None
            for bb in B_ring[s]:
                bb.gen += 1
            P.op("poolq", lambda e: e.dma_start(out=ring[:, s, 0:n], in_=dram_ap_128xN), writes=B_ring[s])
            return s

        def load_k(dram_rows, k, c, slot=None):
            s = next_slot() if slot is None else slot
            src = dram_rows.rearrange("(k p) c -> p k c", p=128)
            dst = ring[:, s, 0:k * c].rearrange("p (k c) -> p k c", c=c)
            for bb in B_ring[s]:
                bb.gen += 1
            P.op("poolq", lambda e: e.dma_start(out=dst, in_=src), writes=B_ring[s])
            return s

        def load_ffn(l, f):
            s = next_slot()
            su = wup_d[l][:, f * 128:(f + 1) * 128].rearrange("(k p) c -> p k c", p=128)
            sg = wup_d[l][:, DFF + f * 128:DFF + (f + 1) * 128].rearrange("(k p) c -> p k c", p=128)
            sd = wdn_d[l][f * 128:(f + 1) * 128, :]
            du = ring[:, s, 0:1024].rearrange("p (k c) -> p k c", c=128)
            dg = ring[:, s, 1024:2048].rearrange("p (k c) -> p k c", c=128)
            dd = ring[:, s, 2048:3072]
            for bb in B_ring[s]:
                bb.gen += 1
            P.op("poolq", lambda e: e.dma_start(out=du, in_=su), writes=[B_ring[s][0]])
            P.op("poolq", lambda e: e.dma_start(out=dg, in_=sg), writes=[B_ring[s][1]])
            P.op("poolq", lambda e: e.dma_start(out=dd, in_=sd), writes=[B_ring[s][2]])
            return (s, B_ring[s][0].gen)

        P.op("act", lambda e: e.activation(out=scT[:], in_=cv[:], func=AF.Silu), reads=[B_cv], writes=[B_scT])
        ada_units = [(l, oc) for l in range(2) for oc in range(48)]
        ada_state = {"loaded": 0, "done": 0}

        def ada_load(n):
            l, oc = ada_units[n]
            j = n % 2
            src = ada_d[l][:, oc * 128:(oc + 1) * 128].rearrange("(k p) c -> p k c", p=128)
            dst = abuf[:, j, :].rearrange("p (k c) -> p k c", c=128)
            P.op("poolq", lambda e: e.dma_start(out=dst, in_=src), writes=[B_ab[j]])

        def ada_step(k):
            for _ in range(k):
                n = ada_state["done"]
                if n >= len(ada_units):
                    return
                while ada_state["loaded"] < min(n + 2, len(ada_units)):
                    ada_load(ada_state["loaded"])
                    ada_state["loaded"] += 1
                l, oc = ada_units[n]
                j = n % 2
                col = (l * 48 + oc) * 2
                for kc in range(8):
                    P.op("pe", (lambda e, kc=kc, j=j, col=col: e.matmul(
                        adabank[:, col:col + 2], lhsT=abuf[:, j, kc * 128:(kc + 1) * 128],
                        rhs=scT[:, kc * 2:kc * 2 + 2], start=(kc == 0), stop=(kc == 7))),
                        reads=[B_ab[j], B_scT], writes=[B_adabank])
                ada_state["done"] = n + 1
                if oc % 8 == 7:
                    ada_evac(l, oc // 8)

        def ada_evac(l, g):
            ab = PV_OFF[f"ada_b{l}"] + g * 8
            c0 = (l * 48 + g * 8) * 2
            for v in range(2):
                P.op("dve", (lambda e, v=v: e.tensor_tensor(
                    out=mod[:, l, v, g * 8:(g + 1) * 8], in0=adabank[:, c0:c0 + 16].rearrange("p (c v) -> p c v", v=2)[:, :, v],
                    in1=pv[:, ab:ab + 8], op=ALU.add)), reads=[B_adabank, B_pv], writes=[B_modg[l][g]])
                if g in (1, 4):
                    go = PV_OFF[f"gmix{l}" if g == 1 else f"gffn{l}"]
                    P.op("dve", (lambda e, v=v, go=go: e.scalar_tensor_tensor(
                        out=mod[:, l, v, g * 8:(g + 1) * 8], in0=mod[:, l, v, g * 8:(g + 1) * 8], scalar=1.0,
                        in1=pv[:, go:go + 8], op0=ALU.add, op1=ALU.mult)), reads=[B_modg[l][g], B_pv], writes=[B_modg[l][g]])

        ada0_slot = {}

        def ada0_load(u, slot=None):
            ada0_slot[u] = load_k(ada_d[0][:, u * 384:(u + 1) * 384], 8, 384, slot)

        def ada0_mm(u):
            s_ = ada0_slot[u]
            for oc3 in range(3):
                oc = u * 3 + oc3
                for kc in range(8):
                    P.op("pe", (lambda e, kc=kc, oc3=oc3, oc=oc: e.matmul(
                        adabank[:, oc * 2:oc * 2 + 2], lhsT=ring[:, s_, kc * 384 + oc3 * 128:kc * 384 + oc3 * 128 + 128],
                        rhs=scT[:, kc * 2:kc * 2 + 2], start=(kc == 0), stop=(kc == 7))),
                        reads=B_ring[s_] + [B_scT], writes=[B_adabank])
                if oc % 8 == 7:
                    ada_evac(0, oc // 8)

        ada0_extras = []

        def ada_startup():
            for u in range(6):
                ada0_load(u)
                ada0_mm(u)
            assert state["ring"] == 6

        for k in range(10):
            u = 6 + k
            ada0_extras.append((max(-0.5, k - 1.4), (lambda u=u, k=k: ada0_load(u, 4 + (k % 2)))))
            ada0_extras.append((k + 0.6, (lambda u=u: ada0_mm(u))))
        ada_state["done"] = 48
        ada_state["loaded"] = 48

        ada_rate = {"sconv": 0}
        ada_limit = {"ffn": 64}

        def modc(l, v, idx):
            return mod[:, l, v, idx:idx + 1]

        def mm_group(T, w_of_kc, rhs_buf, rhs_bufs, wbufs, nk=8, rhs_of=None):
            outs = []
            for (off, n) in T.subs:
                bk, Bk = next_bank()
                outs.append((bk, Bk, off, n))
            for kc in range(nk):
                for (bk, Bk, off, n) in outs:
                    c0 = T.col0 + off
                    if rhs_of is None:
                        rhs = rhs_buf[:, kc, c0:c0 + n]
                    else:
                        rhs = rhs_of(kc, off, n)
                    P.op("pe", (lambda e, bk=bk, n=n, kc=kc, rhs=rhs, w=w_of_kc(kc): e.matmul(
                        bk[:, 0:n], lhsT=w, rhs=rhs, start=(kc == 0), stop=(kc == nk - 1))),
                        reads=wbufs(kc) + [rhs_bufs[kc]], writes=[Bk])
            return outs

        def load_block(T, slot, dram, ntok, b):
            nt = min(128, ntok - b * 128)
            si = b % 2
            P.op("sp", (lambda e: e.dma_start(out=stage[0:nt, si, :], in_=dram[b * 128:b * 128 + nt, :])),
                 writes=[B_stage[si]])
            for g in range(2):
                bk, Bk = next_bank()
                for cc in range(4):
                    c = g * 4 + cc
                    P.op("pe", (lambda e, bk=bk, cc=cc, c=c: e.transpose(
                        bk[:, cc * 128:cc * 128 + nt], stage[0:nt, si, c * 128:(c + 1) * 128], identf[0:nt, 0:nt])),
                        reads=[B_stage[si], B_const], writes=[Bk])
                col = T.col0 + b * 128
                src = bk[:, :].rearrange("p (c t) -> p c t", t=128)[:, :, 0:nt]
                dst = xT[:, g * 4:g * 4 + 4, col:col + nt]
                if g == 0:
                    P.op("act", (lambda e, src=src, dst=dst: e.copy(out=dst, in_=src)), reads=[Bk],
                         writes=B_X[slot][g * 4:g * 4 + 4])
                else:
                    P.op("dve", (lambda e, src=src, dst=dst: e.tensor_copy(out=dst, in_=src)), reads=[Bk],
                         writes=B_X[slot][g * 4:g * 4 + 4])

        def load_xT(T, slot, dram, ntok):
            for b in range((ntok + 127) // 128):
                load_block(T, slot, dram, ntok, b)

        def norm(T, slot, Gi, Si, l, out_t, out_bufs, final=False):
            v = T.v
            c0 = T.col0
            n = T.n
            outs = []
            for (off, nn) in T.subs:
                bk, Bk = next_bank()
                outs.append((bk, Bk, off, nn))
            for c in range(8):
                qi = c % 2
                P.op("act", (lambda e, c=c, qi=qi: e.activation(out=sq[:, qi, 0:n], in_=xT[:, c, c0:c0 + n],
                                                                 func=AF.Square, scale=1.0 / 32.0)),
                     reads=[B_X[slot][c]], writes=[B_sq[qi]])
                for (bk, Bk, off, nn) in outs:
                    P.op("pe", (lambda e, bk=bk, nn=nn, off=off, qi=qi, c=c: e.matmul(
                        bk[:, 0:nn], lhsT=onesb[:], rhs=sq[:, qi, off:off + nn], start=(c == 0), stop=(c == 7))),
                        reads=[B_sq[qi], B_const], writes=[Bk])
            for (bk, Bk, off, nn) in outs:
                P.op("act", (lambda e, bk=bk, off=off, nn=nn: e.activation(out=rstd[:, off:off + nn], in_=bk[:, 0:nn],
                                                                           func=AF.Ln, bias=epst[:], scale=1.0)),
                     reads=[Bk, B_const], writes=[B_rstd])
            P.op("act", lambda e: e.activation(out=rstd[:, 0:n], in_=rstd[:, 0:n], func=AF.Exp, scale=-0.5),
                 reads=[B_rstd], writes=[B_rstd])
            return

        def norm_apply(T, slot, l, Gi, Si, out_t, out_bufs, ocol0):
            v = T.v
            c0 = T.col0
            n = T.n
            for c in range(8):
                ei = c % 2
                P.op("dve", (lambda e, c=c, ei=ei: e.tensor_tensor(out=E[:, ei, 0:n], in0=xT[:, c, c0:c0 + n],
                                                                     in1=rstd[:, 0:n], op=ALU.mult)),
                     reads=[B_X[slot][c], B_rstd], writes=[B_E[ei]])
                P.op("act", (lambda e, c=c, ei=ei: e.activation(out=out_t[:, c, ocol0:ocol0 + n], in_=E[:, ei, 0:n],
                                                                 func=AF.Identity, bias=modc(l, v, Si + c),
                                                                 scale=modc(l, v, Gi + c))),
                     reads=[B_E[ei], B_modg[l][Gi // 8], B_modg[l][Si // 8]], writes=[out_bufs[c]])

        def conv_center(T, src_i, dst_i, w1, bb):
            n = T.n
            P.op("act", (lambda e: e.activation(out=E[:, dst_i, 0:n], in_=E[:, src_i, 0:n], func=AF.Identity, bias=bb, scale=w1)),
                 reads=[B_E[src_i], B_pv], writes=[B_E[dst_i]])

        def conv_taps(T, src_i, dst_i, w0, w2):
            segs = T.segs
            if len(segs) == 2:
                L = segs[0][1]
                sv = E[:, src_i, 0:2 * L].rearrange("p (s t) -> p s t", s=2)
                dv = E[:, dst_i, 0:2 * L].rearrange("p (s t) -> p s t", s=2)
                pairs = [(dv[:, :, 1:L], sv[:, :, 0:L - 1], w0), (dv[:, :, 0:L - 1], sv[:, :, 1:L], w2)]
            else:
                (a, b) = segs[0]
                pairs = [(E[:, dst_i, a + 1:b], E[:, src_i, a:b - 1], w0), (E[:, dst_i, a:b - 1], E[:, src_i, a + 1:b], w2)]
            for (o, i0, w) in pairs:
                P.op("dve", (lambda e, o=o, i0=i0, w=w: e.scalar_tensor_tensor(out=o, in0=i0, scalar=w, in1=o,
                                                                             op0=ALU.mult, op1=ALU.add)),
                     reads=[B_E[src_i], B_E[dst_i], B_pv], writes=[B_E[dst_i]])

        def resid_update(T, slot, outs, oc, gate, gbuf):
            c0 = T.col0
            for (bk, Bk, off, n) in outs:
                P.op("dve", (lambda e, bk=bk, off=off, n=n, oc=oc: e.scalar_tensor_tensor(
                    out=xT[:, oc, c0 + off:c0 + off + n], in0=bk[:, 0:n], scalar=gate,
                    in1=xT[:, oc, c0 + off:c0 + off + n], op0=ALU.mult, op1=ALU.add)),
                    reads=[Bk, gbuf, B_X[slot][oc]], writes=[B_X[slot][oc]])

        def run_sched(items):
            for _, _, fn in sorted(items, key=lambda it: (it[0], it[1])):
                fn()

        def phase_sconv(tiles, extras=()):
            def load_win(c):
                s_ = next_slot()
                for bb in B_ring[s_]:
                    bb.gen += 1
                for j in range(3):
                    src = win_d[:, j * 1024 + c * 128:j * 1024 + (c + 1) * 128].rearrange("(k p) c -> p k c", p=128)
                    dst = ring[:, s_, j * 1024:(j + 1) * 1024].rearrange("p (k c) -> p k c", c=128)
                    P.op("poolq", (lambda e, src=src, dst=dst: e.dma_start(out=dst, in_=src)), writes=[B_ring[s_][j]])
                return s_
            slots = [load_win(c) for c in range(8)]
            items = []
            seq = [0]

            def add(t, fn):
                items.append((t, seq[0], fn))
                seq[0] += 1
            for (t_, fn_) in extras:
                add(t_, fn_)
            g = 0
            for (T, slot) in tiles:
                def stN(T=T, slot=slot):
                    norm(T, slot, 8, 0, 0, tmpT, B_HH[slot])
                    norm_apply(T, slot, 0, 8, 0, tmpT, B_HH[slot], T.col0)
                add(-1.0 if g == 0 else g - 2.5, stN)
                for c in range(8):
                    eb = 2 + (g % 2) * 2

                    def stA(T=T, slot=slot, c=c, eb=eb):
                        v = T.v
                        Tl = T
                        o_cg = mm_group(Tl, lambda kc: ring[:, slots[c], 1024 + kc * 128:1024 + (kc + 1) * 128],
                                        tmpT, B_HH[slot], lambda kc: [B_ring[slots[c]][1]])
                        for (bk, Bk, off, nn) in o_cg:
                            P.op("act", (lambda e, bk=bk, off=off, nn=nn: e.copy(out=E[:, eb, off:off + nn], in_=bk[:, 0:nn])),
                                 reads=[Bk], writes=[B_E[eb]])
                        o_xv = mm_group(Tl, lambda kc: ring[:, slots[c], 2048 + kc * 128:2048 + (kc + 1) * 128],
                                        tmpT, B_HH[slot], lambda kc: [B_ring[slots[c]][2]])
                        for (bk, Bk, off, nn) in o_xv:
                            P.op("dve", (lambda e, bk=bk, off=off, nn=nn: e.tensor_tensor(
                                out=E[:, eb, off:off + nn], in0=E[:, eb, off:off + nn], in1=bk[:, 0:nn], op=ALU.mult)),
                                reads=[Bk, B_E[eb]], writes=[B_E[eb]])
                        conv_center(T, eb, eb + 1, pvc(f"scw_{v}", 8 + c), pvc("scb", c))
                        conv_taps(T, eb, eb + 1, pvc(f"scw_{v}", c), pvc(f"scw_{v}", 16 + c))
                        ada_step(ada_rate["sconv"])

                    def stB(T=T, slot=slot, c=c, eb=eb):
                        o_bg = mm_group(T, lambda kc: ring[:, slots[c], kc * 128:(kc + 1) * 128],
                                        tmpT, B_HH[slot], lambda kc: [B_ring[slots[c]][0]])
                        for (bk, Bk, off, nn) in o_bg:
                            P.op("dve", (lambda e, bk=bk, off=off, nn=nn: e.tensor_tensor(
                                out=actT[:, c, T.col0 + off:T.col0 + off + nn], in0=E[:, eb + 1, off:off + nn], in1=bk[:, 0:nn],
                                op=ALU.mult)), reads=[Bk, B_E[eb + 1]], writes=[B_A[slot][c]])
                    add(g, stA)
                    add(g + 1.5, stB)
                    g += 1
            run_sched(items)

        def phase_proj(tiles, w_d, l, next_norm=True):
            u = [load_k(w_d[0:384, :], 3, 1024), load_k(w_d[384:768, :], 3, 1024), load_k(w_d[768:1024, :], 2, 1024)]
            for (T, slot) in tiles:
                v = T.v
                for oc in range(8):
                    outs = mm_group(T, lambda kc, oc=oc: ring[:, u[kc // 3], (kc % 3) * 1024 + oc * 128:(kc % 3) * 1024 + (oc + 1) * 128],
                                    actT, B_A[slot], lambda kc: B_ring[u[kc // 3]])
                    resid_update(T, slot, outs, oc, modc(l, v, 16 + oc), B_modg[l][2])
                norm(T, slot, 32, 24, l, actT, B_A[slot])
                norm_apply(T, slot, l, 32, 24, actT, B_A[slot], T.col0)

        pre_qkv = []
        attn_normed = []

        def phase_ffn(tiles, l, extras=()):
            groups = [list(range(0, 6)), list(range(6, 12)), list(range(12, 17)), list(range(17, 22))]
            items = []
            seq = [0]

            def add(t, fn):
                items.append((t, seq[0], fn))
                seq[0] += 1
            for (t_, fn_) in extras:
                add(t_, fn_)
            gch = 0
            for gi, grp in enumerate(groups):
                first = gch
                slots = {}
                if l == 0 and gi == len(groups) - 1:
                    def pfq():
                        pre_qkv.extend(load_full(wqkv_d[kc * 128:(kc + 1) * 128, :]) for kc in range(5))
                    add(first + (3.0 if len(tiles) > 1 else 2.0), pfq)
                for j, f in enumerate(grp):
                    t = first + j - 3.5
                    if j >= 4:
                        t = max(t, first + (1.8 if len(tiles) > 1 else 0.8) + 0.01 * j)

                    def ld(f=f, slots=slots):
                        slots[f] = load_ffn(l, f)
                    add(t, ld)
                for (T, slot) in tiles:
                    for fi, f in enumerate(grp):
                        eb = 2 + (gch % 2) * 2
                        box = {}

                        def stA(T=T, slot=slot, f=f, fi=fi, eb=eb, slots=slots, box=box):
                            s = slots[f]
                            v = T.v
                            o_u = mm_group(T, lambda kc: ring[:, s[0], kc * 128:(kc + 1) * 128], actT, B_A[slot],
                                           lambda kc: [(B_ring[s[0]][0], s[1])])
                            for (bk, Bk, off, nn) in o_u:
                                P.op("act", (lambda e, bk=bk, off=off, nn=nn: e.copy(out=E[:, eb, off:off + nn], in_=bk[:, 0:nn])),
                                     reads=[Bk], writes=[B_E[eb]])
                            conv_center(T, eb, eb + 1, pvc(f"fcw{l}_{v}", NF + f), pvc(f"fcb{l}", f))
                            conv_taps(T, eb, eb + 1, pvc(f"fcw{l}_{v}", f), pvc(f"fcw{l}_{v}", 2 * NF + f))
                            if ada_state["done"] < 64:
                                ada_step(1)

                        def stG(T=T, eb=eb):
                            nT = T.n
                            P.op("act", (lambda e: e.activation(out=E[:, eb, 0:nT], in_=E[:, eb + 1, 0:nT], func=AF.Gelu)),
                                 reads=[B_E[eb + 1]], writes=[B_E[eb]])

                        def stB(T=T, slot=slot, f=f, fi=fi, eb=eb, slots=slots):
                            s = slots[f]
                            o_g = mm_group(T, lambda kc: ring[:, s[0], 1024 + kc * 128:1024 + (kc + 1) * 128], actT, B_A[slot],
                                           lambda kc: [(B_ring[s[0]][1], s[1])])
                            for (bk, Bk, off, nn) in o_g:
                                P.op("dve", (lambda e, bk=bk, off=off, nn=nn: e.tensor_tensor(
                                    out=tmpT[:, fi, T.col0 + off:T.col0 + off + nn], in0=E[:, eb, off:off + nn], in1=bk[:, 0:nn], op=ALU.mult)),
                                    reads=[Bk, B_E[eb]], writes=[B_HH[slot][fi]])
                        add(gch, stA)
                        add(gch + 1.2, stG)
                        add(gch + 1.5, stB)
                        gch += 1

                    def down(T=T, slot=slot, grp=grp, slots=slots):
                        Tl = T
                        ng = len(grp)
                        for oc in range(8):
                            outs = mm_group(Tl, lambda kc: ring[:, slots[grp[kc]][0], 2048 + oc * 128:2048 + (oc + 1) * 128],
                                            tmpT, B_HH[slot], lambda kc: [(B_ring[slots[grp[kc]][0]][2], slots[grp[kc]][1])], nk=ng)
                            resid_update(T, slot, outs, oc, modc(l, T.v, 40 + oc), B_modg[l][5])
                    add(gch - 1 + (2.7 if len(tiles) > 1 else 1.7), down)
                    if l == 0 and gi == len(groups) - 1:
                        def attn_norm(T=T, slot=slot):
                            norm(T, slot, 8, 0, 1, tmpT, B_HH[slot])
                            norm_apply(T, slot, 1, 8, 0, tmpT, B_HH[slot], T.col0)
                            attn_normed.append(slot)
                        add(gch - 1 + (2.75 if len(tiles) > 1 else 1.75), attn_norm)
            run_sched(items)

        def ctx_dma(kt):
            for (src_d, si) in ((kc_d, 0), (vc_d, 1)):
                src = src_d[:, kt * 128:(kt + 1) * 128, :].rearrange("h p d -> p h d")
                dst = stage[:, si, :].rearrange("p (h d) -> p h d", d=64)
                P.op("sp", (lambda e, src=src, dst=dst: e.dma_start(out=dst, in_=src)), writes=[B_stage[si]])

        def ctx_compute(kt):
            for g in range(2):
                bk, Bk = next_bank()
                for cc in range(4):
                    c = g * 4 + cc
                    P.op("pe", (lambda e, bk=bk, cc=cc, c=c: e.transpose(
                        bk[:, cc * 128:(cc + 1) * 128], stage[:, 0, c * 128:(c + 1) * 128], identf[:])),
                        reads=[B_stage[0], B_const], writes=[Bk])
                P.op("act", (lambda e, bk=bk, g=g: e.copy(
                    out=KcT[:, g * 4:g * 4 + 4, kt * 128:(kt + 1) * 128],
                    in_=bk[:, :].rearrange("p (c t) -> p c t", t=128))), reads=[Bk], writes=[B_KcT])
            P.op("dve", (lambda e: e.tensor_copy(
                out=VcA[:, kt, :, 0:64], in_=stage[:, 1, :].rearrange("p (h d) -> p h d", d=64))),
                reads=[B_stage[1]], writes=[B_VcA])

        ctx_extras = [(3.0, lambda: ctx_dma(0)), (8.0, lambda: ctx_compute(0)), (8.1, lambda: ctx_dma(1)), (13.0, lambda: ctx_compute(1))]

        def qkv_qk(T, slot, slots, c, prompt, pp):
            Tl = T
            B_H = B_HH[slot]
            Tq = Tl if prompt else Tile(0, [(0, 512), (512, 16)], [], T.v)
            o_q = mm_group(Tq, lambda kc: ring[:, slots[kc], c * 128:(c + 1) * 128], tmpT, B_H, lambda kc: B_ring[slots[kc]])
            for (bk, Bk, off, nn) in o_q:
                for hq in range(2):
                    P.op("act", (lambda e, bk=bk, off=off, nn=nn, hq=hq: e.mul(
                        out=QTz[hq * 64:(hq + 1) * 64, pp, hq, off:off + nn], in_=bk[hq * 64:(hq + 1) * 64, 0:nn], mul=0.125)),
                        reads=[Bk], writes=[B_QT[pp]])
            o_k = mm_group(Tl, lambda kc: ring[:, slots[kc], 1024 + c * 128:1024 + (c + 1) * 128], tmpT, B_H,
                           lambda kc: B_ring[slots[kc]])
            for (bk, Bk, off, nn) in o_k:
                P.op("dve", (lambda e, bk=bk, off=off, nn=nn: e.tensor_copy(out=KT[:, pp, off:off + nn], in_=bk[:, 0:nn])),
                     reads=[Bk], writes=[B_KT[pp]])

        def qkv_v(T, slot, slots, c, prompt, tileidx, pp):
            B_H = B_HH[slot]
            h0 = T.col0
            ntok = T.n
            nblk = (ntok + 127) // 128
            for b0 in range(0, nblk, 4):
                bk, Bk = next_bank()
                blks = list(range(b0, min(b0 + 4, nblk)))
                for bi, b in enumerate(blks):
                    nt = min(128, ntok - b * 128)
                    for kc in range(8):
                        P.op("pe", (lambda e, bk=bk, bi=bi, b=b, nt=nt, kc=kc: e.matmul(
                            bk[0:nt, bi * 128:(bi + 1) * 128], lhsT=tmpT[:, kc, h0 + b * 128:h0 + b * 128 + nt],
                            rhs=ring[:, slots[kc], 2048 + c * 128:2048 + (c + 1) * 128], start=(kc == 0), stop=(kc == 7))),
                            reads=B_ring[slots[kc]] + [B_H[kc]], writes=[Bk])
                for bi, b in enumerate(blks):
                    nt = min(128, ntok - b * 128)
                    P.op("dve", (lambda e, bk=bk, bi=bi, b=b, nt=nt: e.tensor_copy(
                        out=Vaug[0:nt, pp, b, :, 0:64], in_=bk[0:nt, bi * 128:(bi + 1) * 128].rearrange("p (h d) -> p h d", d=64))),
                        reads=[Bk], writes=[B_Vaug[pp]])
                if prompt:
                    nb = len(blks)
                    P.op("dve", (lambda e, bk=bk, nb=nb: e.tensor_copy(out=stage[:, 1, pp * 512:pp * 512 + nb * 128], in_=bk[:, 0:nb * 128])),
                         reads=[Bk], writes=[B_kv[1][pp]])
            if prompt:
                bk, Bk = next_bank()
                for b in range(4):
                    for kc in range(8):
                        P.op("pe", (lambda e, bk=bk, b=b, kc=kc: e.matmul(
                            bk[:, b * 128:(b + 1) * 128], lhsT=tmpT[:, kc, h0 + b * 128:h0 + (b + 1) * 128],
                            rhs=ring[:, slots[kc], 1024 + c * 128:1024 + (c + 1) * 128], start=(kc == 0), stop=(kc == 7))),
                            reads=B_ring[slots[kc]] + [B_H[kc]], writes=[Bk])
                P.op("dve", (lambda e, bk=bk: e.tensor_copy(out=stage[:, 0, pp * 512:(pp + 1) * 512], in_=bk[:, :])),
                     reads=[Bk], writes=[B_kv[0][pp]])
                for which, dd in ((0, sk_d), (1, sv_d)):
                    for b in range(4):
                        seq = tileidx * 2 + b // 2
                        t0 = (b % 2) * 128
                        dst = dd[seq, 2 * c:2 * c + 2, t0:t0 + 128, :].rearrange("h p d -> p h d")
                        src = stage[:, which, pp * 512 + b * 128:pp * 512 + (b + 1) * 128].rearrange("p (h d) -> p h d", d=64)
                        P.op("sp", (lambda e, dst=dst, src=src: e.dma_start(out=dst, in_=src)), reads=[B_kv[which][pp]], out_dma=True)

        def finish_pv(bkpv, Bkpv, nslots, hh, pp):
            pvv = bkpv[:, 0:nslots * 96].rearrange("p (m f) -> p m f", f=96)
            P.op("dve", (lambda e: e.reciprocal(out=rc[:, hh, 0:nslots], in_=pvv[:, :, 64])),
                 reads=[Bkpv], writes=[B_rc[hh]])
            rb = rc[:, hh, 0:nslots].rearrange("p (m o) -> p m o", o=1).broadcast_to([128, nslots, 64])
            P.op("dve", (lambda e: e.tensor_tensor(out=Otok[:, pp, 0:nslots, hh * 64:(hh + 1) * 64], in0=pvv[:, :, 0:64],
                                                     in1=rb, op=ALU.mult)),
                 reads=[Bkpv, B_rc[hh]], writes=[B_Otok[pp]])

        def otok_to_actT(T, slot, c, blocks, pp):
            bf, Bf = next_bf()
            for (m, nt, col) in blocks:
                P.op("pe", (lambda e, m=m, nt=nt, col=col: e.transpose(bf[:, col:col + nt], Otok[0:nt, pp, m, :], identb[0:nt, 0:nt])),
                     reads=[B_Otok[pp], B_const], writes=[Bf])
            ntot = blocks[-1][2] + blocks[-1][1]
            P.op("dve", (lambda e: e.tensor_copy(out=actT[:, c, T.col0:T.col0 + ntot], in_=bf[:, 0:ntot])),
                 reads=[Bf], writes=[B_A[slot][c]])

        def next_pt():
            i = state["pt"]
            state["pt"] = (i + 1) % NPT
            B_PT[i].gen += 1
            return i

        def ptb(i):
            return (B_PT[i], B_PT[i].gen)

        def prompt_B(c, pp):
            res = []
            for hh in range(2):
                hp0 = hh * 64
                for s2 in range(2):
                    bk, Bk = next_bank()
                    for kt in range(2):
                        P.op("pe", (lambda e, bk=bk, kt=kt, s2=s2, hh=hh: e.matmul(
                            bk[:, kt * 256:(kt + 1) * 256], lhsT=KT[:, pp, s2 * 256 + kt * 128:s2 * 256 + (kt + 1) * 128],
                            rhs=QTz[:, pp, hh, s2 * 256:(s2 + 1) * 256], start=True, stop=True)),
                            reads=[B_KT[pp], B_QT[pp]], writes=[Bk])
                    pi = next_pt()
                    P.op("act", (lambda e, bk=bk, pi=pi: e.activation(out=PT[:, pi, :], in_=bk[:, :], func=AF.Exp)),
                         reads=[Bk], writes=[B_PT[pi]])
                    res.append((pi, B_PT[pi].gen))
            return res

        def prompt_C(T, slot, c, pp, res):
            for hh in range(2):
                bkpv, Bkpv = next_bank()
                for s2 in range(2):
                    pi, gen = res[hh * 2 + s2]
                    for qt in range(2):
                        m = s2 * 2 + qt
                        for kt in range(2):
                            P.op("pe", (lambda e, m=m, kt=kt, qt=qt, pi=pi, s2=s2, hh=hh, bkpv=bkpv: e.matmul(
                                bkpv[:, m * 96:m * 96 + 66], lhsT=PT[:, pi, kt * 256 + qt * 128:kt * 256 + (qt + 1) * 128],
                                rhs=Vaug[:, pp, s2 * 2 + kt, hh, :], start=(kt == 0), stop=(kt == 1))),
                                reads=[(B_PT[pi], gen), B_Vaug[pp]], writes=[Bkpv])
                finish_pv(bkpv, Bkpv, 4, hh, pp)

        A_OF_M = {0: [0, 1, 2, 3], 1: [0, 1, 2, 3, 4], 2: [0, 1, 2, 3, 4, 5], 3: [1, 2, 3, 4, 5]}
        Ef = E[:, :, :].rearrange("p a b -> p (a b)")

        def sample_B(c, hh, pp):
            hp0 = hh * 64
            h = 2 * c + hh
            halves = []
            for half in range(2):
                r = state["half"]
                state["half"] = (r + 1) % 3
                hb = [B_E[2 * r], B_E[2 * r + 1]]
                P.op("sp", (lambda e, r=r, half=half: e.dma_start(out=Ef[:, r * 1664:r * 1664 + 1536],
                                                                   in_=bias0_d[h][:, half * 1536:(half + 1) * 1536])),
                     writes=hb)
                halves.append((r, hb))
            pis = []
            for a in range(6):
                bk, Bk = next_bank()
                P.op("pe", (lambda e, bk=bk, a=a: e.matmul(bk[:, :], lhsT=KT[:, pp, a * 128:(a + 1) * 128],
                                                             rhs=QTz[:, pp, hh, 0:512], start=True, stop=True)),
                     reads=[B_KT[pp], B_QT[pp]], writes=[Bk])
                r, hb = halves[a // 3]
                ti = state["tt"]
                state["tt"] = 1 - ti
                bcol = r * 1664 + (a % 3) * 512
                P.op("dve", (lambda e, bk=bk, ti=ti, bcol=bcol: e.tensor_tensor(out=Tt[:, ti, :], in0=bk[:, :],
                                                                                 in1=Ef[:, bcol:bcol + 512], op=ALU.add)),
                     reads=[Bk] + hb, writes=[B_T[ti]])
                pi = next_pt()
                pis.append((pi, B_PT[pi].gen))
                P.op("act", (lambda e, ti=ti, pi=pi: e.activation(out=PT[:, pi, :], in_=Tt[:, ti, :], func=AF.Exp)),
                     reads=[B_T[ti]], writes=[B_PT[pi]])
            for kt in range(2):
                bk, Bk = next_bank()
                P.op("pe", (lambda e, bk=bk, kt=kt: e.matmul(bk[:, :], lhsT=KcT[:, c, kt * 128:(kt + 1) * 128],
                                                               rhs=QTz[:, pp, hh, 0:512], start=True, stop=True)),
                     reads=[B_KcT, B_QT[pp]], writes=[Bk])
                pi = next_pt()
                pis.append((pi, B_PT[pi].gen))
                P.op("act", (lambda e, bk=bk, pi=pi: e.activation(out=PT[:, pi, :], in_=bk[:, :], func=AF.Exp)),
                     reads=[Bk], writes=[B_PT[pi]])
            bk1, Bk1 = next_bank()
            for idx, a in enumerate([2, 3, 4, 5, 6]):
                na = 128 if a < 6 else 64
                P.op("pe", (lambda e, idx=idx, a=a, na=na: e.matmul(bk1[0:na, idx * 16:(idx + 1) * 16],
                                                                     lhsT=KT[:, pp, a * 128:a * 128 + na],
                                                                     rhs=QTz[:, pp, hh, 512:528], start=True, stop=True)),
                     reads=[B_KT[pp], B_QT[pp]], writes=[Bk1])
            for kt in range(2):
                P.op("pe", (lambda e, kt=kt: e.matmul(bk1[:, (5 + kt) * 16:(6 + kt) * 16], lhsT=KcT[:, c, kt * 128:(kt + 1) * 128],
                                                       rhs=QTz[:, pp, hh, 512:528], start=True, stop=True)),
                     reads=[B_KcT, B_QT[pp]], writes=[Bk1])
            b1b = b1[:, h * 5:(h + 1) * 5].rearrange("p (a o) -> p a o", o=1).broadcast_to([128, 5, 16])
            P.op("dve", (lambda e: e.tensor_tensor(out=t80[:, hh, :].rearrange("p (a q) -> p a q", q=16),
                                                     in0=bk1[:, 0:80].rearrange("p (a q) -> p a q", q=16), in1=b1b, op=ALU.add)),
                 reads=[Bk1, B_b1], writes=[B_t80[hh]])
            P.op("act", lambda e: e.activation(out=PT1[:, hh, 0:5, :], in_=t80[:, hh, :].rearrange("p (a q) -> p a q", q=16), func=AF.Exp),
                 reads=[B_t80[hh]], writes=[B_PT1[hh]])
            P.op("act", lambda e: e.activation(out=PT1[:, hh, 5:7, :], in_=bk1[:, 80:112].rearrange("p (a q) -> p a q", q=16), func=AF.Exp),
                 reads=[Bk1], writes=[B_PT1[hh]])
            return pis

        def sample_C(c, hh, pp, pis):
            h = 2 * c + hh
            bkpv, Bkpv = next_bank()
            for m in range(4):
                lst = [("l", a) for a in A_OF_M[m]] + [("c", 0), ("c", 1)]
                for i, (kind, a) in enumerate(lst):
                    if kind == "l":
                        pi, gen = pis[a]
                        P.op("pe", (lambda e, m=m, a=a, i=i, L=len(lst), pi=pi: e.matmul(
                            bkpv[:, m * 96:m * 96 + 66], lhsT=PT[:, pi, m * 128:(m + 1) * 128], rhs=Vaug[:, pp, a, hh, :],
                            start=(i == 0), stop=(i == L - 1))), reads=[(B_PT[pi], gen), B_Vaug[pp]], writes=[Bkpv])
                    else:
                        pi, gen = pis[6 + a]
                        P.op("pe", (lambda e, m=m, a=a, i=i, L=len(lst), pi=pi: e.matmul(
                            bkpv[:, m * 96:m * 96 + 66], lhsT=PT[:, pi, m * 128:(m + 1) * 128], rhs=VcA[:, a, h, :],
                            start=(i == 0), stop=(i == L - 1))), reads=[(B_PT[pi], gen), B_VcA], writes=[Bkpv])
            lst = [("l", 2), ("l", 3), ("l", 4), ("l", 5), ("l", 6), ("c", 0), ("c", 1)]
            for i, (kind, a) in enumerate(lst):
                if kind == "l":
                    na = 128 if a < 6 else 64
                    P.op("pe", (lambda e, a=a, i=i, na=na: e.matmul(
                        bkpv[0:16, 4 * 96:4 * 96 + 66], lhsT=PT1[0:na, hh, a - 2, :], rhs=Vaug[0:na, pp, a, hh, :],
                        start=(i == 0), stop=(i == 6))), reads=[B_PT1[hh], B_Vaug[pp]], writes=[Bkpv])
                else:
                    P.op("pe", (lambda e, a=a, i=i: e.matmul(
                        bkpv[0:16, 4 * 96:4 * 96 + 66], lhsT=PT1[:, hh, 5 + a, :], rhs=VcA[:, a, h, :],
                        start=(i == 0), stop=(i == 6))), reads=[B_PT1[hh], B_VcA], writes=[Bkpv])
            finish_pv(bkpv, Bkpv, 5, hh, pp)

        def phase_attn(tiles, prompt):
            slots = list(pre_qkv)
            del pre_qkv[:]
            for kc in range(len(slots), 8):
                slots.append(load_full(wqkv_d[kc * 128:(kc + 1) * 128, :]))
            for ti, (T, slot) in enumerate(tiles):
                if slot in attn_normed:
                    continue
                norm(T, slot, 8, 0, 1, tmpT, B_HH[slot])
                norm_apply(T, slot, 1, 8, 0, tmpT, B_HH[slot], T.col0)
            del attn_normed[:]
            pairs = [(ti, T, slot, c) for ti, (T, slot) in enumerate(tiles) for c in range(8)]
            n = len(pairs)

            def A_qk(i):
                ti, T, slot, c = pairs[i]
                qkv_qk(T, slot, slots, c, prompt, i % 2)

            def A_v(i):
                ti, T, slot, c = pairs[i]
                qkv_v(T, slot, slots, c, prompt, slot, i % 2)
            def fence():
                allkv = B_kv[0] + B_kv[1]
                P.op("dve", lambda e: e.memset(rc[:, 0, 7:8], 0.0), writes=allkv + B_stage + [B_rc[0]])

            def C2(i):
                ti, T, slot, c = pairs[i]
                blocks = [(m, 128, m * 128) for m in range(4)]
                if not prompt:
                    blocks = blocks + [(4, 16, 512)]
                otok_to_actT(T, slot, c, blocks, i % 2)
            if prompt:
                fence()
                A_qk(0)
                A_v(0)
                res = {0: prompt_B(pairs[0][3], 0)}
                if n > 1:
                    A_qk(1)
                    A_v(1)
                for i in range(n):
                    ti, T, slot, c = pairs[i]
                    prompt_C(T, slot, c, i % 2, res.pop(i))
                    if i + 1 < n:
                        res[i + 1] = prompt_B(pairs[i + 1][3], (i + 1) % 2)
                    if i + 2 < n:
                        A_qk(i + 2)
                        A_v(i + 2)
                    C2(i)
                fence()
            else:
                A_qk(0)
                A_v(0)
                for i in range(n):
                    ti, T, slot, c = pairs[i]
                    pp = i % 2
                    pis0 = sample_B(c, 0, pp)
                    ada_step(2)
                    if i > 0:
                        C2(i - 1)
                    if i + 1 < n:
                        A_qk(i + 1)
                    sample_C(c, 0, pp, pis0)
                    pis1 = sample_B(c, 1, pp)
                    ada_step(2)
                    if i + 1 < n:
                        A_v(i + 1)
                    sample_C(c, 1, pp, pis1)
                C2(n - 1)

        def final_out(T, slot, dram, ntok):
            norm(T, slot, 0, 0, 0, None, None)
            c0 = T.col0
            nblk = ntok // 128
            for b in range(nblk):
                si = b % 2
                for g in range(2):
                    bk, Bk = next_bank()
                    for cc in range(4):
                        c = g * 4 + cc
                        ei = 4 + (cc % 2)
                        col = c0 + b * 128
                        P.op("dve", (lambda e, c=c, ei=ei, col=col, b=b: e.scalar_tensor_tensor(
                            out=E[:, ei, 0:128], in0=xT[:, c, col:col + 128], scalar=pvc("gfin", c),
                            in1=rstd[:, b * 128:(b + 1) * 128], op0=ALU.mult, op1=ALU.mult)),
                            reads=[B_X[slot][c], B_rstd, B_pv], writes=[B_E[ei]])
                        P.op("pe", (lambda e, bk=bk, cc=cc, ei=ei: e.transpose(bk[:, cc * 128:(cc + 1) * 128], E[:, ei, 0:128], identf[:])),
                             reads=[B_E[ei], B_const], writes=[Bk])
                    P.op("act", (lambda e, bk=bk, si=si, g=g: e.copy(out=stage[:, si, g * 512:(g + 1) * 512], in_=bk[:, :])),
                         reads=[Bk], writes=[B_stage[si]])
                P.op("sp", (lambda e, si=si, b=b: e.dma_start(out=dram[b * 128:(b + 1) * 128, :], in_=stage[:, si, :])),
                     reads=[B_stage[si]], out_dma=True)

        def dbg_dump(ntok):
            for c in range(8):
                P.op("sp", (lambda e, c=c: e.dma_start(out=dbg_d[:, c, 0:ntok], in_=xT[:, c, 0:ntok])),
                     reads=[B_X[0][c], B_X[1][c]], out_dma=True)

        TS = Tile(0, [(0, 512), (512, 320)], [(0, 832)], 1)
        TS2 = Tile(0, [(0, 512), (512, 16)], [(0, 528)], 1)
        TS3 = Tile(0, [(0, 512)], [(0, 512)], 1)
        TP = [Tile(0, [(0, 512)], [(0, 256), (256, 512)], 0), Tile(528, [(0, 512)], [(0, 256), (256, 512)], 0)]
        tl = [(TP[1], 1), (TP[0], 0)]

        def fence_p1():
            P.op("dve", lambda e: e.memset(rc[:, 1, 7:8], 0.0), reads=B_X[0] + B_A[0] + B_HH[0],
                 writes=B_X[1] + B_A[1] + B_HH[1] + [B_rc[1]])
        p1_extras = [(0.5, fence_p1)] + [(3.0 + 4.0 * b, (lambda b=b: load_block(TP[1], 1, xp_d[512:1024, :], 512, b))) for b in range(4)]
        steps = [
            ("S_load", lambda: load_xT(TS, 0, xs_d, 832)),
            ("ada0", ada_startup),
            ("S_sconv", lambda: phase_sconv([(TS, 0)], ada0_extras)),
            ("S_proj0", lambda: phase_proj([(TS, 0)], wout_d, 0)),
            ("S_l0", lambda: phase_ffn([(TS, 0)], 0, ctx_extras)),
            ("S_ctx", lambda: None),
            ("S_attn", lambda: phase_attn([(TS, 0)], False)),
            ("S_proj1", lambda: phase_proj([(TS2, 0)], wo_d, 1)),
            ("S_l1", lambda: phase_ffn([(TS2, 0)], 1, p1_extras)),
            ("S_fin", lambda: final_out(TS3, 0, ys_d, 512)),
            ("P_load", lambda: None),
            ("P_sconv", lambda: phase_sconv(tl, [(0.5 + 1.0 * b, (lambda b=b: load_block(TP[0], 0, xp_d[0:512, :], 512, b))) for b in range(4)])),
            ("P_proj0", lambda: phase_proj(tl, wout_d, 0)),
            ("P_l0", lambda: phase_ffn(tl, 0)),
            ("P_attn", lambda: phase_attn(tl, True)),
            ("P_proj1", lambda: phase_proj(tl, wo_d, 1)),
            ("P_l1", lambda: phase_ffn(tl, 1)),
            ("P_fin", lambda: (final_out(TP[0], 0, yp_d[0:512, :], 512), final_out(TP[1], 1, yp_d[512:1024, :], 512))),
        ]
        for name, fn in steps:
            if name == "P_load":
                assert ada_state["done"] == 96
                state["nb"] = NB + 1
            fn()
            if DEBUG == name:
                dbg_dump(1024)
                break
        P.emit()
    return nc


def _fm(vec):
    return np.ascontiguousarray(np.asarray(vec, np.float32).reshape(-1, 128).T)


def _bias_tables(rpb, parity):
    rpb = np.asarray(rpb, np.float32)

    def table(krow_l, kcol_l, qrow_l, qcol_l):
        if parity:
            krow, kcol, qrow, qcol = 15 - krow_l, 63 - kcol_l, 15 - qrow_l, 63 - qcol_l
        else:
            krow, kcol, qrow, qcol = krow_l, kcol_l, qrow_l, qcol_l
        start = np.clip(qrow - 4, 0, 8)
        cs = np.clip(qcol - 8, 0, 48)
        valid = (krow >= start) & (krow < start + 8) & (kcol >= cs) & (kcol < cs + 16) & (krow >= 0) & (krow <= 15)
        dr = np.clip(krow - qrow + 7, 0, 14)
        dc = np.clip(kcol - qcol, -15, 15) + 15
        valid, dr, dc = np.broadcast_arrays(valid, dr, dc)
        out = rpb[:, dr, dc]
        return np.where(valid[None], out, np.float32(NEG)).astype(np.float32)

    k = np.arange(128)[:, None, None]
    a = np.arange(6)[None, :, None]
    q = np.arange(512)[None, None, :]
    ktok = a * 128 + k
    b0 = table(ktok // 64, ktok % 64, q // 64, q % 64)
    b0 = np.ascontiguousarray(b0.reshape(16, 128, 3072))
    a1 = np.arange(2, 7)[None, :, None]
    q1 = 512 + np.arange(16)[None, None, :]
    ktok1 = a1 * 128 + k
    b1 = table(ktok1 // 64, ktok1 % 64, q1 // 64, q1 % 64)
    b1 = np.ascontiguousarray(np.transpose(b1[:, :, :, 0], (1, 0, 2)).reshape(128, 16 * 5))
    return b0, b1


_NC_CACHE = {}


def kernel(x_prompt, x_sample, cache_k_ctx, cache_v_ctx, c, c_ctx, ada_w, ada_b, norm_mix_g, norm_ffn_g,
           sc_w_in, sc_conv_w, sc_conv_b, sc_w_out, na_w_qkv, na_rpb, na_w_o, ffn_w_up, ffn_conv_w,
           ffn_conv_b, ffn_w_down, final_g):
    f = lambda a: np.ascontiguousarray(np.asarray(a, np.float32))
    x_prompt, x_sample = f(x_prompt), f(x_sample)
    key = DEBUG
    if key not in _NC_CACHE:
        _NC_CACHE[key] = build_nc()
    nc = _NC_CACHE[key]
    tabs = [_bias_tables(na_rpb[0], p) for p in range(2)]
    shared = {"ada_w": f(ada_w), "sc_w_in": f(sc_w_in[0]), "sc_w_out": f(sc_w_out[0]), "na_w_qkv": f(na_w_qkv[0]),
              "na_w_o": f(na_w_o[0]), "ffn_w_up": f(ffn_w_up), "ffn_w_down": f(ffn_w_down)}
    in_maps = []
    for core in range(8):
        s, par = core // 2, core % 2
        xs_full = x_sample[s]
        xs_loc = xs_full[::-1][:832] if par else xs_full[:832]
        pvv = np.zeros((128, PV_N), np.float32)

        def put(name, arr):
            arr = _fm(arr)
            pvv[:, PV_OFF[name]:PV_OFF[name] + arr.shape[1]] = arr
        for l in range(2):
            put(f"ada_b{l}", ada_b[l])
            put(f"gmix{l}", norm_mix_g[l])
            put(f"gffn{l}", norm_ffn_g[l])
            fw = np.asarray(ffn_conv_w[l], np.float32)
            put(f"fcw{l}_0", fw.reshape(-1))
            put(f"fcw{l}_1", (fw[::-1] if par else fw).reshape(-1))
            put(f"fcb{l}", ffn_conv_b[l])
        sw = np.asarray(sc_conv_w[0], np.float32)
        put("scw_0", sw.reshape(-1))
        put("scw_1", (sw[::-1] if par else sw).reshape(-1))
        put("scb", sc_conv_b[0])
        put("gfin", final_g)
        cvv = np.stack([_fm(c_ctx), _fm(c[s])], axis=2).reshape(128, 16)
        m = {"xp": np.ascontiguousarray(x_prompt[4 * core:4 * core + 4].reshape(1024, D)),
             "xs": np.ascontiguousarray(xs_loc), "kc": f(cache_k_ctx[s, 0]), "vc": f(cache_v_ctx[s, 0]),
             "cvec": np.ascontiguousarray(cvv, np.float32), "pvec": pvv, "bias0": tabs[par][0], "bias1": tabs[par][1]}
        m.update(shared)
        in_maps.append(m)
    res = run_bass_kernel_spmd(nc, in_maps, core_ids=list(range(8)))
    rs = res.results
    y_prompt = np.empty((32, 256, D), np.float32)
    y_sample = np.empty((4, 1024, D), np.float32)
    sk = np.empty((32, 1, H, 256, 64), np.float32)
    sv = np.empty((32, 1, H, 256, 64), np.float32)
    for core in range(8):
        s, par = core // 2, core % 2
        r = rs[core]
        y_prompt[4 * core:4 * core + 4] = r["yp"].reshape(4, 256, D)
        if par:
            y_sample[s, 512:] = r["ys"][::-1]
        else:
            y_sample[s, :512] = r["ys"]
        sk[4 * core:4 * core + 4, 0] = r["sk"]
        sv[4 * core:4 * core + 4, 0] = r["sv"]
    if DEBUG:
        kernel.dbg = [rs[i]["dbg"] for i in range(8)]
    return (y_prompt, y_sample, sk, sv)
```

```python
import numpy as np
from contextlib import ExitStack
import concourse.bass as bass
import concourse.mybir as mybir
from concourse.bass_utils import run_bass_kernel_spmd

F32 = mybir.dt.float32
BF16 = mybir.dt.bfloat16
AF = mybir.ActivationFunctionType
ALU = mybir.AluOpType

D = 1024
DFF = 2816
NF = 22
H = 16
NEG = -30000.0
DEBUG = None
ATT_MODE = 9

COMPUTE = ("pe", "act", "dve", "pool")
QUEUES = ("sp", "actq", "poolq")
STREAM = {"pe": "tensor", "act": "scalar", "dve": "vector", "pool": "gpsimd",
          "sp": "sync", "actq": "scalar", "poolq": "gpsimd"}


class Buf:
    __slots__ = ("name", "last_w", "readers", "wsem", "wcount", "rsem", "rcount", "gen")

    def __init__(self, name):
        self.name = name
        self.gen = 0
        self.last_w = None
        self.readers = []
        self.wsem = None
        self.wcount = 0
        self.rsem = None
        self.rcount = 0


class Op:
    __slots__ = ("eng", "fn", "deps", "signal", "sig_idx", "is_dma", "dma_sem", "dma_target", "stream")

    def __init__(self, eng, fn):
        self.eng = eng
        self.fn = fn
        self.deps = []
        self.signal = False
        self.sig_idx = None
        self.is_dma = eng in QUEUES
        self.dma_sem = None
        self.dma_target = None
        self.stream = STREAM[eng]


class Prog:
    def __init__(self, nc, stack):
        self.nc = nc
        self.stack = stack
        self.ops = []
        self.final_waits = []
        self.nsem = 0

    def _sem(self, name):
        self.nsem += 1
        return self.stack.enter_context(self.nc.semaphore(name))

    def op(self, eng, fn, reads=(), writes=(), out_dma=False):
        o = Op(eng, fn)
        rr = []
        for b in reads:
            if isinstance(b, tuple):
                assert b[0].gen == b[1], f"stale ring/pt slot {b[0].name}: gen {b[0].gen} != {b[1]}"
                b = b[0]
            rr.append(b)
        reads = rr
        deps = []
        for b in reads:
            if b.last_w is not None:
                deps.append(b.last_w)
        for b in writes:
            if b.last_w is not None:
                deps.append(b.last_w)
            deps.extend(b.readers)
        seen = set()
        for d in deps:
            if id(d) in seen:
                continue
            seen.add(id(d))
            if d.eng == "pe" and eng == "pe":
                continue
            if not d.is_dma:
                d.signal = True
            o.deps.append(d)
        if o.is_dma:
            if out_dma:
                b = reads[0]
                if b.rsem is None:
                    b.rsem = self._sem("r_" + b.name)
                b.rcount += 16
                o.dma_sem, o.dma_target = b.rsem, b.rcount
                self.final_waits.append(o)
            else:
                b = writes[0]
                if b.wsem is None:
                    b.wsem = self._sem("w_" + b.name)
                b.wcount += 16
                o.dma_sem, o.dma_target = b.wsem, b.wcount
        for b in writes:
            b.last_w = o
            b.readers = []
        for b in reads:
            b.readers.append(o)
        self.ops.append(o)
        return o

    def emit(self):
        nc = self.nc
        esem = {e: self._sem("e_" + e) for e in COMPUTE}
        cnt = {e: 0 for e in COMPUTE}
        for o in self.ops:
            if not o.is_dma and o.signal:
                cnt[o.eng] += 1
                o.sig_idx = cnt[o.eng]
        streams = {"tensor": [], "scalar": [], "vector": [], "gpsimd": [], "sync": []}
        for o in self.ops:
            streams[o.stream].append(o)
        final = list(self.final_waits)

        def run(engine, ops, is_sync=False):
            waited = {}

            def wait(sem, val):
                k = id(sem)
                if waited.get(k, 0) >= val:
                    return
                waited[k] = val
                engine.wait_ge(sem, val)

            for o in ops:
                need = {}
                for d in o.deps:
                    sem, val = (d.dma_sem, d.dma_target) if d.is_dma else (esem[d.eng], d.sig_idx)
                    k = id(sem)
                    if k not in need or need[k][1] < val:
                        need[k] = (sem, val)
                for sem, val in need.values():
                    wait(sem, val)
                ins = o.fn(engine)
                if o.is_dma:
                    ins.then_inc(o.dma_sem, 16)
                elif o.signal:
                    ins.then_inc(esem[o.eng], 1)
            if is_sync:
                last = {}
                for o in final:
                    k = id(o.dma_sem)
                    if k not in last or last[k][1] < o.dma_target:
                        last[k] = (o.dma_sem, o.dma_target)
                for sem, val in last.values():
                    engine.wait_ge(sem, val)

        with nc.Block() as block:
            @block.tensor
            def _(e):
                run(e, streams["tensor"])

            @block.scalar
            def _(e):
                run(e, streams["scalar"])

            @block.vector
            def _(e):
                run(e, streams["vector"])

            @block.gpsimd
            def _(e):
                run(e, streams["gpsimd"])

            @block.sync
            def _(e):
                run(e, streams["sync"], is_sync=True)


def _pv_layout():
    off = {}
    n = 0

    def add(name, cnt):
        nonlocal n
        off[name] = n
        n += cnt
    for l in range(2):
        add(f"ada_b{l}", 48)
        add(f"gmix{l}", 8)
        add(f"gffn{l}", 8)
        for v in range(2):
            add(f"fcw{l}_{v}", 3 * NF)
        add(f"fcb{l}", NF)
    for v in range(2):
        add(f"scw_{v}", 24)
    add("scb", 8)
    add("gfin", 8)
    return off, n


PV_OFF, PV_N = _pv_layout()


class Tile:
    def __init__(self, col0, subs, segs, v):
        self.col0 = col0
        self.subs = subs
        self.segs = segs
        self.v = v
        self.n = sum(s[1] for s in subs)


def build_nc():
    nc = bass.Bass("TRN2", target_bir_lowering=False)

    def din(name, shape):
        return nc.dram_tensor(name, list(shape), F32, kind="ExternalInput").ap()

    def dout(name, shape):
        return nc.dram_tensor(name, list(shape), F32, kind="ExternalOutput").ap()

    xp_d = din("xp", [1024, D])
    xs_d = din("xs", [832, D])
    kc_d = din("kc", [H, 256, 64])
    vc_d = din("vc", [H, 256, 64])
    cvec_d = din("cvec", [128, 16])
    pvec_d = din("pvec", [128, PV_N])
    bias0_d = din("bias0", [H, 128, 3072])
    bias1_d = din("bias1", [128, H * 5])
    ada_d = din("ada_w", [2, D, 6 * D])
    win_d = din("sc_w_in", [D, 3 * D])
    wout_d = din("sc_w_out", [D, D])
    wqkv_d = din("na_w_qkv", [D, 3 * D])
    wo_d = din("na_w_o", [D, D])
    wup_d = din("ffn_w_up", [2, D, 2 * DFF])
    wdn_d = din("ffn_w_down", [2, DFF, D])
    yp_d = dout("yp", [1024, D])
    ys_d = dout("ys", [512, D])
    sk_d = dout("sk", [4, H, 256, 64])
    sv_d = dout("sv", [4, H, 256, 64])
    dbg_d = dout("dbg", [128, 8, 1024]) if DEBUG else None

    with ExitStack() as st:
        P = Prog(nc, st)

        def sb(name, shape, dt):
            return st.enter_context(nc.sbuf_tensor(name, list(shape), dt))

        def psum(name, shape, dt):
            return st.enter_context(nc.psum_tensor(name, list(shape), dt))

        R = 10
        NPT = 12
        xT = sb("xT", [128, 8, 1040], F32)
        actT = sb("actT", [128, 8, 1040], BF16)
        tmpT = sb("tmpT", [128, 8, 1040], BF16)
        ring = sb("ring", [128, R, 3072], BF16)
        E = sb("E", [128, 6, 832], F32)
        rstd = sb("rstd", [128, 832], F32)
        sq = sb("sq", [128, 2, 832], BF16)
        pv = sb("pv", [128, PV_N], F32)
        cv = sb("cv", [128, 16], F32)
        scT = sb("scT", [128, 16], BF16)
        mod = sb("mod", [128, 2, 2, 48], F32)
        stage = sb("stage", [128, 2, 1024], F32)
        identf = sb("identf", [128, 128], F32)
        identb = sb("identb", [128, 128], BF16)
        onesb = sb("onesb", [128, 128], BF16)
        epst = sb("epst", [128, 1], F32)
        QTz = sb("QTz", [128, 2, 2, 528], BF16)
        KT = sb("KT", [128, 2, 832], BF16)
        KcT = sb("KcT", [128, 8, 256], BF16)
        VcA = sb("VcA", [128, 2, H, 66], BF16)
        Vaug = sb("Vaug", [128, 2, 7, 2, 66], BF16)
        PT = sb("PT", [128, NPT, 512], BF16)
        PT1 = sb("PT1", [128, 2, 7, 16], BF16)
        Tt = sb("Tt", [128, 2, 512], F32)
        t80 = sb("t80", [128, 2, 80], F32)
        b1 = sb("b1", [128, H * 5], F32)
        Otok = sb("Otok", [128, 2, 5, 128], BF16)
        rc = sb("rc", [128, 2, 8], F32)
        abuf = sb("abuf", [128, 2, 1024], BF16)

        NB = 6
        banks = [psum(f"bank{i}", [128, 512], F32) for i in range(NB)]
        bankbf = [psum(f"bankbf{i}", [128, 1024], BF16) for i in range(1)]
        adabank = psum("adabank", [128, 512], F32)
        B_adabank = Buf("adabank")
        B_bank = [Buf(f"bank{i}") for i in range(NB)]
        B_bf = [Buf(f"bf{i}") for i in range(1)]
        state = {"bank": 0, "bf": 0, "ring": 0, "pt": 0, "half": 0, "tt": 0, "nb": NB}
        banks.append(adabank)
        B_bank.append(B_adabank)

        def next_bank():
            i = state["bank"]
            state["bank"] = (i + 1) % state["nb"]
            return banks[i], B_bank[i]

        def next_bf():
            i = state["bf"]
            state["bf"] = 0
            return bankbf[0], B_bf[0]

        B_ring = [[Buf(f"ring{i}_{j}") for j in range(3)] for i in range(R)]

        def next_slot():
            i = state["ring"]
            state["ring"] = (i + 1) % R
            return i

        B_x = [Buf(f"x{c}") for c in range(8)]
        B_X = [[Buf(f"X{t}_{c}") for c in range(8)] for t in range(2)]
        B_A = [[Buf(f"A{t}_{c}") for c in range(8)] for t in range(2)]
        B_HH = [[Buf(f"H{t}_{c}") for c in range(8)] for t in range(2)]
        B_E = [Buf(f"E{i}") for i in range(6)]
        B_rstd = Buf("rstd")
        B_sq = [Buf("sq0"), Buf("sq1")]
        B_pv = Buf("pv")
        B_cv = Buf("cv")
        B_scT = Buf("scT")
        B_modg = [[Buf(f"mod{l}_{g}") for g in range(6)] for l in range(2)]
        B_stage = [Buf("stage0"), Buf("stage1")]
        B_kv = [[Buf("kst0"), Buf("kst1")], [Buf("vst0"), Buf("vst1")]]
        B_const = Buf("const")
        B_QT = [Buf("QT0"), Buf("QT1")]
        B_KT = [Buf("KT0"), Buf("KT1")]
        B_KcT = Buf("KcT")
        B_VcA = Buf("VcA")
        B_Vaug = [Buf("Vaug0"), Buf("Vaug1")]
        B_PT = [Buf(f"PT{i}") for i in range(NPT)]
        B_PT1 = [Buf("PT1a"), Buf("PT1b")]
        B_T = [Buf("T0"), Buf("T1")]
        B_t80 = [Buf("t80a"), Buf("t80b")]
        B_b1 = Buf("b1")
        B_Otok = [Buf("Otok0"), Buf("Otok1")]
        B_rc = [Buf("rc0"), Buf("rc1")]
        B_ab = [Buf("ab0"), Buf("ab1")]

        P.op("pool", lambda e: e.memset(identf[:], 1.0), writes=[B_const])
        P.op("pool", lambda e: e.affine_select(out=identf[:], in_=identf[:], pattern=[[-1, 128]],
                                               compare_op=ALU.is_equal, fill=0.0, base=0, channel_multiplier=1),
             reads=[B_const], writes=[B_const])
        P.op("dve", lambda e: e.tensor_copy(out=identb[:], in_=identf[:]), reads=[B_const], writes=[B_const])
        P.op("dve", lambda e: e.memset(onesb[:], 1.0), writes=[B_const])
        P.op("dve", lambda e: e.memset(epst[:], 1e-6), writes=[B_const])
        P.op("dve", lambda e: e.memset(Vaug[:], 1.0), writes=B_Vaug)
        P.op("dve", lambda e: e.memset(QTz[:], 0.0), writes=B_QT)
        P.op("dve", lambda e: e.memset(VcA[:], 1.0), writes=[B_VcA])
        P.op("sp", lambda e: e.dma_start(out=pv[:], in_=pvec_d), writes=[B_pv])
        P.op("sp", lambda e: e.dma_start(out=cv[:], in_=cvec_d), writes=[B_cv])
        P.op("sp", lambda e: e.dma_start(out=b1[:], in_=bias1_d), writes=[B_b1])

        def pvc(name, i=0):
            o = PV_OFF[name] + i
            return pv[:, o:o + 1]

        def ring_ap(s, a, b):
            return ring[:, s, a:b]

        def load_full(dram_ap_128xN):
            s = next_slot()
            n = dram_ap_128xN.shape[-1] if len(dram_ap_128xN.shape) == 2 else None
            for bb in B_ring[s]:
                bb.gen += 1
            P.op("poolq", lambda e: e.dma_start(out=ring[:, s, 0:n], in_=dram_ap_128xN), writes=B_ring[s])
            return s

        def load_k(dram_rows, k, c, slot=None):
            s = next_slot() if slot is None else slot
            src = dram_rows.rearrange("(k p) c -> p k c", p=128)
            dst = ring[:, s, 0:k * c].rearrange("p (k c) -> p k c", c=c)
            for bb in B_ring[s]:
                bb.gen += 1
            P.op("poolq", lambda e: e.dma_start(out=dst, in_=src), writes=B_ring[s])
            return s

        def load_ffn(l, f):
            s = next_slot()
            su = wup_d[l][:, f * 128:(f + 1) * 128].rearrange("(k p) c -> p k c", p=128)
            sg = wup_d[l][:, DFF + f * 128:DFF + (f + 1) * 128].rearrange("(k p) c -> p k c", p=128)
            sd = wdn_d[l][f * 128:(f + 1) * 128, :]
            du = ring[:, s, 0:1024].rearrange("p (k c) -> p k c", c=128)
            dg = ring[:, s, 1024:2048].rearrange("p (k c) -> p k c", c=128)
            dd = ring[:, s, 2048:3072]
            for bb in B_ring[s]:
                bb.gen += 1
            P.op("poolq", lambda e: e.dma_start(out=du, in_=su), writes=[B_ring[s][0]])
            P.op("poolq", lambda e: e.dma_start(out=dg, in_=sg), writes=[B_ring[s][1]])
            P.op("poolq", lambda e: e.dma_start(out=dd, in_=sd), writes=[B_ring[s][2]])
            return (s, B_ring[s][0].gen)

        P.op("act", lambda e: e.activation(out=scT[:], in_=cv[:], func=AF.Silu), reads=[B_cv], writes=[B_scT])
        ada_units = [(l, oc) for l in range(2) for oc in range(48)]
        ada_state = {"loaded": 0, "done": 0}

        def ada_load(n):
            l, oc = ada_units[n]
            j = n % 2
            src = ada_d[l][:, oc * 128:(oc + 1) * 128].rearrange("(k p) c -> p k c", p=128)
            dst = abuf[:, j, :].rearrange("p (k c) -> p k c", c=128)
            P.op("poolq", lambda e: e.dma_start(out=dst, in_=src), writes=[B_ab[j]])

        def ada_step(k):
            for _ in range(k):
                n = ada_state["done"]
                if n >= len(ada_units):
                    return
                while ada_state["loaded"] < min(n + 2, len(ada_units)):
                    ada_load(ada_state["loaded"])
                    ada_state["loaded"] += 1
                l, oc = ada_units[n]
                j = n % 2
                col = (l * 48 + oc) * 2
                for kc in range(8):
                    P.op("pe", (lambda e, kc=kc, j=j, col=col: e.matmul(
                        adabank[:, col:col + 2], lhsT=abuf[:, j, kc * 128:(kc + 1) * 128],
                        rhs=scT[:, kc * 2:kc * 2 + 2], start=(kc == 0), stop=(kc == 7))),
                        reads=[B_ab[j], B_scT], writes=[B_adabank])
                ada_state["done"] = n + 1
                if oc % 8 == 7:
                    ada_evac(l, oc // 8)

        def ada_evac(l, g):
            ab = PV_OFF[f"ada_b{l}"] + g * 8
            c0 = (l * 48 + g * 8) * 2
            for v in range(2):
                P.op("dve", (lambda e, v=v: e.tensor_tensor(
                    out=mod[:, l, v, g * 8:(g + 1) * 8], in0=adabank[:, c0:c0 + 16].rearrange("p (c v) -> p c v", v=2)[:, :, v],
                    in1=pv[:, ab:ab + 8], op=ALU.add)), reads=[B_adabank, B_pv], writes=[B_modg[l][g]])
                if g in (1, 4):
                    go = PV_OFF[f"gmix{l}" if g == 1 else f"gffn{l}"]
                    P.op("dve", (lambda e, v=v, go=go: e.scalar_tensor_tensor(
                        out=mod[:, l, v, g * 8:(g + 1) * 8], in0=mod[:, l, v, g * 8:(g + 1) * 8], scalar=1.0,
                        in1=pv[:, go:go + 8], op0=ALU.add, op1=ALU.mult)), reads=[B_modg[l][g], B_pv], writes=[B_modg[l][g]])

        ada0_slot = {}

        def ada0_load(u, slot=None):
            ada0_slot[u] = load_k(ada_d[0][:, u * 384:(u + 1) * 384], 8, 384, slot)

        def ada0_mm(u):
            s_ = ada0_slot[u]
            for oc3 in range(3):
                oc = u * 3 + oc3
                for kc in range(8):
                    P.op("pe", (lambda e, kc=kc, oc3=oc3, oc=oc: e.matmul(
                        adabank[:, oc * 2:oc * 2 + 2], lhsT=ring[:, s_, kc * 384 + oc3 * 128:kc * 384 + oc3 * 128 + 128],
                        rhs=scT[:, kc * 2:kc * 2 + 2], start=(kc == 0), stop=(kc == 7))),
                        reads=B_ring[s_] + [B_scT], writes=[B_adabank])
                if oc % 8 == 7:
                    ada_evac(0, oc // 8)

        ada0_extras = []

        def ada_startup():
            for u in range(6):
                ada0_load(u)
                ada0_mm(u)
            assert state["ring"] == 6

        for k in range(10):
            u = 6 + k
            ada0_extras.append((max(-0.5, k - 1.4), (lambda u=u, k=k: ada0_load(u, 4 + (k % 2)))))
            ada0_extras.append((k + 0.6, (lambda u=u: ada0_mm(u))))
        ada_state["done"] = 48
        ada_state["loaded"] = 48

        ada_rate = {"sconv": 0}
        ada_limit = {"ffn": 64}

        def modc(l, v, idx):
            return mod[:, l, v, idx:idx + 1]

        def mm_group(T, w_of_kc, rhs_buf, rhs_bufs, wbufs, nk=8, rhs_of=None):
            outs = []
            for (off, n) in T.subs:
                bk, Bk = next_bank()
                outs.append((bk, Bk, off, n))
            for kc in range(nk):
                for (bk, Bk, off, n) in outs:
                    c0 = T.col0 + off
                    if rhs_of is None:
                        rhs = rhs_buf[:, kc, c0:c0 + n]
                    else:
                        rhs = rhs_of(kc, off, n)
                    P.op("pe", (lambda e, bk=bk, n=n, kc=kc, rhs=rhs, w=w_of_kc(kc): e.matmul(
                        bk[:, 0:n], lhsT=w, rhs=rhs, start=(kc == 0), stop=(kc == nk - 1))),
                        reads=wbufs(kc) + [rhs_bufs[kc]], writes=[Bk])
            return outs

        def load_block(T, slot, dram, ntok, b, stg=None):
            nt = min(128, ntok - b * 128)
            si = b % 2
            if stg is None:
                sview, sbufs = stage[:, si, :], [B_stage[si]]
            else:
                sview, sbufs = stg
            P.op("sp", (lambda e: e.dma_start(out=sview[0:nt, :], in_=dram[b * 128:b * 128 + nt, :])),
                 writes=sbufs)
            for g in range(2):
                bk, Bk = next_bank()
                for cc in range(4):
                    c = g * 4 + cc
                    P.op("pe", (lambda e, bk=bk, cc=cc, c=c: e.transpose(
                        bk[:, cc * 128:cc * 128 + nt], sview[0:nt, c * 128:(c + 1) * 128], identf[0:nt, 0:nt])),
                        reads=sbufs + [B_const], writes=[Bk])
                col = T.col0 + b * 128
                src = bk[:, :].rearrange("p (c t) -> p c t", t=128)[:, :, 0:nt]
                dst = xT[:, g * 4:g * 4 + 4, col:col + nt]
                if g == 0:
                    P.op("act", (lambda e, src=src, dst=dst: e.copy(out=dst, in_=src)), reads=[Bk],
                         writes=B_X[slot][g * 4:g * 4 + 4])
                else:
                    P.op("dve", (lambda e, src=src, dst=dst: e.tensor_copy(out=dst, in_=src)), reads=[Bk],
                         writes=B_X[slot][g * 4:g * 4 + 4])

        def load_xT(T, slot, dram, ntok):
            for b in range((ntok + 127) // 128):
                load_block(T, slot, dram, ntok, b)

        def norm(T, slot, Gi, Si, l, out_t, out_bufs, final=False):
            v = T.v
            c0 = T.col0
            n = T.n
            outs = []
            for (off, nn) in T.subs:
                bk, Bk = next_bank()
                outs.append((bk, Bk, off, nn))
            for c in range(8):
                qi = c % 2
                P.op("act", (lambda e, c=c, qi=qi: e.activation(out=sq[:, qi, 0:n], in_=xT[:, c, c0:c0 + n],
                                                                 func=AF.Square, scale=1.0 / 32.0)),
                     reads=[B_X[slot][c]], writes=[B_sq[qi]])
                for (bk, Bk, off, nn) in outs:
                    P.op("pe", (lambda e, bk=bk, nn=nn, off=off, qi=qi, c=c: e.matmul(
                        bk[:, 0:nn], lhsT=onesb[:], rhs=sq[:, qi, off:off + nn], start=(c == 0), stop=(c == 7))),
                        reads=[B_sq[qi], B_const], writes=[Bk])
            for (bk, Bk, off, nn) in outs:
                P.op("act", (lambda e, bk=bk, off=off, nn=nn: e.activation(out=rstd[:, off:off + nn], in_=bk[:, 0:nn],
                                                                           func=AF.Ln, bias=epst[:], scale=1.0)),
                     reads=[Bk, B_const], writes=[B_rstd])
            P.op("act", lambda e: e.activation(out=rstd[:, 0:n], in_=rstd[:, 0:n], func=AF.Exp, scale=-0.5),
                 reads=[B_rstd], writes=[B_rstd])
            return

        def norm_apply(T, slot, l, Gi, Si, out_t, out_bufs, ocol0):
            v = T.v
            c0 = T.col0
            n = T.n
            for c in range(8):
                ei = c % 2
                P.op("dve", (lambda e, c=c, ei=ei: e.tensor_tensor(out=E[:, ei, 0:n], in0=xT[:, c, c0:c0 + n],
                                                                     in1=rstd[:, 0:n], op=ALU.mult)),
                     reads=[B_X[slot][c], B_rstd], writes=[B_E[ei]])
                P.op("act", (lambda e, c=c, ei=ei: e.activation(out=out_t[:, c, ocol0:ocol0 + n], in_=E[:, ei, 0:n],
                                                                 func=AF.Identity, bias=modc(l, v, Si + c),
                                                                 scale=modc(l, v, Gi + c))),
                     reads=[B_E[ei], B_modg[l][Gi // 8], B_modg[l][Si // 8]], writes=[out_bufs[c]])

        def conv_center(T, src_i, dst_i, w1, bb):
            n = T.n
            P.op("act", (lambda e: e.activation(out=E[:, dst_i, 0:n], in_=E[:, src_i, 0:n], func=AF.Identity, bias=bb, scale=w1)),
                 reads=[B_E[src_i], B_pv], writes=[B_E[dst_i]])

        def conv_taps(T, src_i, dst_i, w0, w2):
            segs = T.segs
            if len(segs) == 2:
                L = segs[0][1]
                sv = E[:, src_i, 0:2 * L].rearrange("p (s t) -> p s t", s=2)
                dv = E[:, dst_i, 0:2 * L].rearrange("p (s t) -> p s t", s=2)
                pairs = [(dv[:, :, 1:L], sv[:, :, 0:L - 1], w0), (dv[:, :, 0:L - 1], sv[:, :, 1:L], w2)]
            else:
                (a, b) = segs[0]
                pairs = [(E[:, dst_i, a + 1:b], E[:, src_i, a:b - 1], w0), (E[:, dst_i, a:b - 1], E[:, src_i, a + 1:b], w2)]
            for (o, i0, w) in pairs:
                P.op("dve", (lambda e, o=o, i0=i0, w=w: e.scalar_tensor_tensor(out=o, in0=i0, scalar=w, in1=o,
                                                                             op0=ALU.mult, op1=ALU.add)),
                     reads=[B_E[src_i], B_E[dst_i], B_pv], writes=[B_E[dst_i]])

        def resid_update(T, slot, outs, oc, gate, gbuf):
            c0 = T.col0
            for (bk, Bk, off, n) in outs:
                P.op("dve", (lambda e, bk=bk, off=off, n=n, oc=oc: e.scalar_tensor_tensor(
                    out=xT[:, oc, c0 + off:c0 + off + n], in0=bk[:, 0:n], scalar=gate,
                    in1=xT[:, oc, c0 + off:c0 + off + n], op0=ALU.mult, op1=ALU.add)),
                    reads=[Bk, gbuf, B_X[slot][oc]], writes=[B_X[slot][oc]])

        def run_sched(items):
            for _, _, fn in sorted(items, key=lambda it: (it[0], it[1])):
                fn()

        def phase_sconv(tiles, extras=()):
            def load_win(c):
                s_ = next_slot()
                for bb in B_ring[s_]:
                    bb.gen += 1
                for j in range(3):
                    src = win_d[:, j * 1024 + c * 128:j * 1024 + (c + 1) * 128].rearrange("(k p) c -> p k c", p=128)
                    dst = ring[:, s_, j * 1024:(j + 1) * 1024].rearrange("p (k c) -> p k c", c=128)
                    P.op("poolq", (lambda e, src=src, dst=dst: e.dma_start(out=dst, in_=src)), writes=[B_ring[s_][j]])
                return s_
            slots = [load_win(c) for c in range(8)]
            items = []
            seq = [0]

            def add(t, fn):
                items.append((t, seq[0], fn))
                seq[0] += 1
            for (t_, fn_) in extras:
                add(t_, fn_)
            g = 0
            for (T, slot) in tiles:
                def stN(T=T, slot=slot):
                    norm(T, slot, 8, 0, 0, tmpT, B_HH[slot])
                    norm_apply(T, slot, 0, 8, 0, tmpT, B_HH[slot], T.col0)
                add(-1.0 if g == 0 else g - 2.5, stN)
                for c in range(8):
                    eb = 2 + (g % 2) * 2

                    def stA(T=T, slot=slot, c=c, eb=eb):
                        v = T.v
                        Tl = T
                        o_cg = mm_group(Tl, lambda kc: ring[:, slots[c], 1024 + kc * 128:1024 + (kc + 1) * 128],
                                        tmpT, B_HH[slot], lambda kc: [B_ring[slots[c]][1]])
                        for (bk, Bk, off, nn) in o_cg:
                            P.op("act", (lambda e, bk=bk, off=off, nn=nn: e.copy(out=E[:, eb, off:off + nn], in_=bk[:, 0:nn])),
                                 reads=[Bk], writes=[B_E[eb]])
                        o_xv = mm_group(Tl, lambda kc: ring[:, slots[c], 2048 + kc * 128:2048 + (kc + 1) * 128],
                                        tmpT, B_HH[slot], lambda kc: [B_ring[slots[c]][2]])
                        for (bk, Bk, off, nn) in o_xv:
                            P.op("dve", (lambda e, bk=bk, off=off, nn=nn: e.tensor_tensor(
                                out=E[:, eb, off:off + nn], in0=E[:, eb, off:off + nn], in1=bk[:, 0:nn], op=ALU.mult)),
                                reads=[Bk, B_E[eb]], writes=[B_E[eb]])
                        conv_center(T, eb, eb + 1, pvc(f"scw_{v}", 8 + c), pvc("scb", c))
                        conv_taps(T, eb, eb + 1, pvc(f"scw_{v}", c), pvc(f"scw_{v}", 16 + c))
                        ada_step(ada_rate["sconv"])

                    def stB(T=T, slot=slot, c=c, eb=eb):
                        o_bg = mm_group(T, lambda kc: ring[:, slots[c], kc * 128:(kc + 1) * 128],
                                        tmpT, B_HH[slot], lambda kc: [B_ring[slots[c]][0]])
                        for (bk, Bk, off, nn) in o_bg:
                            P.op("dve", (lambda e, bk=bk, off=off, nn=nn: e.tensor_tensor(
                                out=actT[:, c, T.col0 + off:T.col0 + off + nn], in0=E[:, eb + 1, off:off + nn], in1=bk[:, 0:nn],
                                op=ALU.mult)), reads=[Bk, B_E[eb + 1]], writes=[B_A[slot][c]])
                    add(g, stA)
                    add(g + 1.5, stB)
                    g += 1
            run_sched(items)

        def phase_proj(tiles, w_d, l, next_norm=True):
            u = [load_k(w_d[0:384, :], 3, 1024), load_k(w_d[384:768, :], 3, 1024), load_k(w_d[768:1024, :], 2, 1024)]
            for (T, slot) in tiles:
                v = T.v
                for oc in range(8):
                    outs = mm_group(T, lambda kc, oc=oc: ring[:, u[kc // 3], (kc % 3) * 1024 + oc * 128:(kc % 3) * 1024 + (oc + 1) * 128],
                                    actT, B_A[slot], lambda kc: B_ring[u[kc // 3]])
                    resid_update(T, slot, outs, oc, modc(l, v, 16 + oc), B_modg[l][2])
                norm(T, slot, 32, 24, l, actT, B_A[slot])
                norm_apply(T, slot, l, 32, 24, actT, B_A[slot], T.col0)

        pre_qkv = []
        attn_normed = []

        def phase_ffn(tiles, l, extras=()):
            groups = [list(range(0, 6)), list(range(6, 12)), list(range(12, 17)), list(range(17, 22))]
            items = []
            seq = [0]

            def add(t, fn):
                items.append((t, seq[0], fn))
                seq[0] += 1
            for (t_, fn_) in extras:
                add(t_, fn_)
            gch = 0
            for gi, grp in enumerate(groups):
                first = gch
                slots = {}
                if l == 0 and gi == len(groups) - 1:
                    def pfq():
                        pre_qkv.extend(load_full(wqkv_d[kc * 128:(kc + 1) * 128, :]) for kc in range(5))
                    add(first + (3.0 if len(tiles) > 1 else 2.0), pfq)
                for j, f in enumerate(grp):
                    t = first + j - 3.5
                    if j >= 4:
                        t = max(t, first + (1.8 if len(tiles) > 1 else 0.8) + 0.01 * j)

                    def ld(f=f, slots=slots):
                        slots[f] = load_ffn(l, f)
                    add(t, ld)
                for (T, slot) in tiles:
                    for fi, f in enumerate(grp):
                        eb = 2 + (gch % 2) * 2
                        box = {}

                        def stA(T=T, slot=slot, f=f, fi=fi, eb=eb, slots=slots, box=box):
                            s = slots[f]
                            v = T.v
                            o_u = mm_group(T, lambda kc: ring[:, s[0], kc * 128:(kc + 1) * 128], actT, B_A[slot],
                                           lambda kc: [(B_ring[s[0]][0], s[1])])
                            for (bk, Bk, off, nn) in o_u:
                                P.op("act", (lambda e, bk=bk, off=off, nn=nn: e.copy(out=E[:, eb, off:off + nn], in_=bk[:, 0:nn])),
                                     reads=[Bk], writes=[B_E[eb]])
                            conv_center(T, eb, eb + 1, pvc(f"fcw{l}_{v}", NF + f), pvc(f"fcb{l}", f))
                            conv_taps(T, eb, eb + 1, pvc(f"fcw{l}_{v}", f), pvc(f"fcw{l}_{v}", 2 * NF + f))
                            if ada_state["done"] < 64:
                                ada_step(1)

                        def stG(T=T, eb=eb):
                            nT = T.n
                            P.op("act", (lambda e: e.activation(out=E[:, eb, 0:nT], in_=E[:, eb + 1, 0:nT], func=AF.Gelu)),
                                 reads=[B_E[eb + 1]], writes=[B_E[eb]])

                        def stB(T=T, slot=slot, f=f, fi=fi, eb=eb, slots=slots):
                            s = slots[f]
                            o_g = mm_group(T, lambda kc: ring[:, s[0], 1024 + kc * 128:1024 + (kc + 1) * 128], actT, B_A[slot],
                                           lambda kc: [(B_ring[s[0]][1], s[1])])
                            for (bk, Bk, off, nn) in o_g:
                                P.op("dve", (lambda e, bk=bk, off=off, nn=nn: e.tensor_tensor(
                                    out=tmpT[:, fi, T.col0 + off:T.col0 + off + nn], in0=E[:, eb, off:off + nn], in1=bk[:, 0:nn], op=ALU.mult)),
                                    reads=[Bk, B_E[eb]], writes=[B_HH[slot][fi]])
                        add(gch, stA)
                        add(gch + 1.2, stG)
                        add(gch + 1.5, stB)
                        gch += 1

                    def down(T=T, slot=slot, grp=grp, slots=slots):
                        Tl = T
                        ng = len(grp)
                        for oc in range(8):
                            outs = mm_group(Tl, lambda kc: ring[:, slots[grp[kc]][0], 2048 + oc * 128:2048 + (oc + 1) * 128],
                                            tmpT, B_HH[slot], lambda kc: [(B_ring[slots[grp[kc]][0]][2], slots[grp[kc]][1])], nk=ng)
                            resid_update(T, slot, outs, oc, modc(l, T.v, 40 + oc), B_modg[l][5])
                    add(gch - 1 + (2.7 if len(tiles) > 1 else 1.7), down)
                    if l == 0 and gi == len(groups) - 1:
                        def attn_norm(T=T, slot=slot):
                            norm(T, slot, 8, 0, 1, tmpT, B_HH[slot])
                            norm_apply(T, slot, 1, 8, 0, tmpT, B_HH[slot], T.col0)
                            attn_normed.append(slot)
                        add(gch - 1 + (2.75 if len(tiles) > 1 else 1.75), attn_norm)
            run_sched(items)

        def ctx_dma(kt):
            for (src_d, si) in ((kc_d, 0), (vc_d, 1)):
                src = src_d[:, kt * 128:(kt + 1) * 128, :].rearrange("h p d -> p h d")
                dst = stage[:, si, :].rearrange("p (h d) -> p h d", d=64)
                P.op("sp", (lambda e, src=src, dst=dst: e.dma_start(out=dst, in_=src)), writes=[B_stage[si]])

        def ctx_compute(kt):
            for g in range(2):
                bk, Bk = next_bank()
                for cc in range(4):
                    c = g * 4 + cc
                    P.op("pe", (lambda e, bk=bk, cc=cc, c=c: e.transpose(
                        bk[:, cc * 128:(cc + 1) * 128], stage[:, 0, c * 128:(c + 1) * 128], identf[:])),
                        reads=[B_stage[0], B_const], writes=[Bk])
                P.op("act", (lambda e, bk=bk, g=g: e.copy(
                    out=KcT[:, g * 4:g * 4 + 4, kt * 128:(kt + 1) * 128],
                    in_=bk[:, :].rearrange("p (c t) -> p c t", t=128))), reads=[Bk], writes=[B_KcT])
            P.op("dve", (lambda e: e.tensor_copy(
                out=VcA[:, kt, :, 0:64], in_=stage[:, 1, :].rearrange("p (h d) -> p h d", d=64))),
                reads=[B_stage[1]], writes=[B_VcA])

        ctx_extras = [(3.0, lambda: ctx_dma(0)), (8.0, lambda: ctx_compute(0)), (8.1, lambda: ctx_dma(1)), (13.0, lambda: ctx_compute(1))]

        def qkv_qk(T, slot, slots, c, prompt, pp):
            Tl = T
            B_H = B_HH[slot]
            Tq = Tl if prompt else Tile(0, [(0, 512), (512, 16)], [], T.v)
            o_q = mm_group(Tq, lambda kc: ring[:, slots[kc], c * 128:(c + 1) * 128], tmpT, B_H, lambda kc: B_ring[slots[kc]])
            for (bk, Bk, off, nn) in o_q:
                for hq in range(2):
                    P.op("act", (lambda e, bk=bk, off=off, nn=nn, hq=hq: e.mul(
                        out=QTz[hq * 64:(hq + 1) * 64, pp, hq, off:off + nn], in_=bk[hq * 64:(hq + 1) * 64, 0:nn], mul=0.125)),
                        reads=[Bk], writes=[B_QT[pp]])
            o_k = mm_group(Tl, lambda kc: ring[:, slots[kc], 1024 + c * 128:1024 + (c + 1) * 128], tmpT, B_H,
                           lambda kc: B_ring[slots[kc]])
            for (bk, Bk, off, nn) in o_k:
                P.op("dve", (lambda e, bk=bk, off=off, nn=nn: e.tensor_copy(out=KT[:, pp, off:off + nn], in_=bk[:, 0:nn])),
                     reads=[Bk], writes=[B_KT[pp]])

        def qkv_v(T, slot, slots, c, prompt, tileidx, pp):
            B_H = B_HH[slot]
            h0 = T.col0
            ntok = T.n
            nblk = (ntok + 127) // 128
            for b0 in range(0, nblk, 4):
                bk, Bk = next_bank()
                blks = list(range(b0, min(b0 + 4, nblk)))
                for bi, b in enumerate(blks):
                    nt = min(128, ntok - b * 128)
                    for kc in range(8):
                        P.op("pe", (lambda e, bk=bk, bi=bi, b=b, nt=nt, kc=kc: e.matmul(
                            bk[0:nt, bi * 128:(bi + 1) * 128], lhsT=tmpT[:, kc, h0 + b * 128:h0 + b * 128 + nt],
                            rhs=ring[:, slots[kc], 2048 + c * 128:2048 + (c + 1) * 128], start=(kc == 0), stop=(kc == 7))),
                            reads=B_ring[slots[kc]] + [B_H[kc]], writes=[Bk])
                for bi, b in enumerate(blks):
                    nt = min(128, ntok - b * 128)
                    P.op("dve", (lambda e, bk=bk, bi=bi, b=b, nt=nt: e.tensor_copy(
                        out=Vaug[0:nt, pp, b, :, 0:64], in_=bk[0:nt, bi * 128:(bi + 1) * 128].rearrange("p (h d) -> p h d", d=64))),
                        reads=[Bk], writes=[B_Vaug[pp]])
                if prompt:
                    nb = len(blks)
                    P.op("dve", (lambda e, bk=bk, nb=nb: e.tensor_copy(out=stage[:, 1, pp * 512:pp * 512 + nb * 128], in_=bk[:, 0:nb * 128])),
                         reads=[Bk], writes=[B_kv[1][pp]])
            if prompt:
                bk, Bk = next_bank()
                for b in range(4):
                    for kc in range(8):
                        P.op("pe", (lambda e, bk=bk, b=b, kc=kc: e.matmul(
                            bk[:, b * 128:(b + 1) * 128], lhsT=tmpT[:, kc, h0 + b * 128:h0 + (b + 1) * 128],
                            rhs=ring[:, slots[kc], 1024 + c * 128:1024 + (c + 1) * 128], start=(kc == 0), stop=(kc == 7))),
                            reads=B_ring[slots[kc]] + [B_H[kc]], writes=[Bk])
                P.op("dve", (lambda e, bk=bk: e.tensor_copy(out=stage[:, 0, pp * 512:(pp + 1) * 512], in_=bk[:, :])),
                     reads=[Bk], writes=[B_kv[0][pp]])
                for which, dd in ((0, sk_d), (1, sv_d)):
                    for b in range(4):
                        seq = tileidx * 2 + b // 2
                        t0 = (b % 2) * 128
                        dst = dd[seq, 2 * c:2 * c + 2, t0:t0 + 128, :].rearrange("h p d -> p h d")
                        src = stage[:, which, pp * 512 + b * 128:pp * 512 + (b + 1) * 128].rearrange("p (h d) -> p h d", d=64)
                        P.op("sp", (lambda e, dst=dst, src=src: e.dma_start(out=dst, in_=src)), reads=[B_kv[which][pp]], out_dma=True)

        def finish_pv(bkpv, Bkpv, nslots, hh, pp):
            pvv = bkpv[:, 0:nslots * 96].rearrange("p (m f) -> p m f", f=96)
            P.op("dve", (lambda e: e.reciprocal(out=rc[:, hh, 0:nslots], in_=pvv[:, :, 64])),
                 reads=[Bkpv], writes=[B_rc[hh]])
            rb = rc[:, hh, 0:nslots].rearrange("p (m o) -> p m o", o=1).broadcast_to([128, nslots, 64])
            P.op("dve", (lambda e: e.tensor_tensor(out=Otok[:, pp, 0:nslots, hh * 64:(hh + 1) * 64], in0=pvv[:, :, 0:64],
                                                     in1=rb, op=ALU.mult)),
                 reads=[Bkpv, B_rc[hh]], writes=[B_Otok[pp]])

        def otok_to_actT(T, slot, c, blocks, pp):
            bf, Bf = next_bf()
            for (m, nt, col) in blocks:
                P.op("pe", (lambda e, m=m, nt=nt, col=col: e.transpose(bf[:, col:col + nt], Otok[0:nt, pp, m, :], identb[0:nt, 0:nt])),
                     reads=[B_Otok[pp], B_const], writes=[Bf])
            ntot = blocks[-1][2] + blocks[-1][1]
            P.op("dve", (lambda e: e.tensor_copy(out=actT[:, c, T.col0:T.col0 + ntot], in_=bf[:, 0:ntot])),
                 reads=[Bf], writes=[B_A[slot][c]])

        def next_pt():
            i = state["pt"]
            state["pt"] = (i + 1) % NPT
            B_PT[i].gen += 1
            return i

        def ptb(i):
            return (B_PT[i], B_PT[i].gen)

        def prompt_B(c, pp):
            res = []
            for hh in range(2):
                hp0 = hh * 64
                for s2 in range(2):
                    bk, Bk = next_bank()
                    for kt in range(2):
                        P.op("pe", (lambda e, bk=bk, kt=kt, s2=s2, hh=hh: e.matmul(
                            bk[:, kt * 256:(kt + 1) * 256], lhsT=KT[:, pp, s2 * 256 + kt * 128:s2 * 256 + (kt + 1) * 128],
                            rhs=QTz[:, pp, hh, s2 * 256:(s2 + 1) * 256], start=True, stop=True)),
                            reads=[B_KT[pp], B_QT[pp]], writes=[Bk])
                    pi = next_pt()
                    P.op("act", (lambda e, bk=bk, pi=pi: e.activation(out=PT[:, pi, :], in_=bk[:, :], func=AF.Exp)),
                         reads=[Bk], writes=[B_PT[pi]])
                    res.append((pi, B_PT[pi].gen))
            return res

        def prompt_C(T, slot, c, pp, res):
            for hh in range(2):
                bkpv, Bkpv = next_bank()
                for s2 in range(2):
                    pi, gen = res[hh * 2 + s2]
                    for qt in range(2):
                        m = s2 * 2 + qt
                        for kt in range(2):
                            P.op("pe", (lambda e, m=m, kt=kt, qt=qt, pi=pi, s2=s2, hh=hh, bkpv=bkpv: e.matmul(
                                bkpv[:, m * 96:m * 96 + 66], lhsT=PT[:, pi, kt * 256 + qt * 128:kt * 256 + (qt + 1) * 128],
                                rhs=Vaug[:, pp, s2 * 2 + kt, hh, :], start=(kt == 0), stop=(kt == 1))),
                                reads=[(B_PT[pi], gen), B_Vaug[pp]], writes=[Bkpv])
                finish_pv(bkpv, Bkpv, 4, hh, pp)

        A_OF_M = {0: [0, 1, 2, 3], 1: [0, 1, 2, 3, 4], 2: [0, 1, 2, 3, 4, 5], 3: [1, 2, 3, 4, 5]}
        Ef = E[:, :, :].rearrange("p a b -> p (a b)")

        def sample_B(c, hh, pp):
            hp0 = hh * 64
            h = 2 * c + hh
            halves = []
            for half in range(2):
                r = state["half"]
                state["half"] = (r + 1) % 3
                hb = [B_E[2 * r], B_E[2 * r + 1]]
                P.op("sp", (lambda e, r=r, half=half: e.dma_start(out=Ef[:, r * 1664:r * 1664 + 1536],
                                                                   in_=bias0_d[h][:, half * 1536:(half + 1) * 1536])),
                     writes=hb)
                halves.append((r, hb))
            pis = []
            for a in range(6):
                bk, Bk = next_bank()
                P.op("pe", (lambda e, bk=bk, a=a: e.matmul(bk[:, :], lhsT=KT[:, pp, a * 128:(a + 1) * 128],
                                                             rhs=QTz[:, pp, hh, 0:512], start=True, stop=True)),
                     reads=[B_KT[pp], B_QT[pp]], writes=[Bk])
                r, hb = halves[a // 3]
                ti = state["tt"]
                state["tt"] = 1 - ti
                bcol = r * 1664 + (a % 3) * 512
                P.op("dve", (lambda e, bk=bk, ti=ti, bcol=bcol: e.tensor_tensor(out=Tt[:, ti, :], in0=bk[:, :],
                                                                                 in1=Ef[:, bcol:bcol + 512], op=ALU.add)),
                     reads=[Bk] + hb, writes=[B_T[ti]])
                pi = next_pt()
                pis.append((pi, B_PT[pi].gen))
                P.op("act", (lambda e, ti=ti, pi=pi: e.activation(out=PT[:, pi, :], in_=Tt[:, ti, :], func=AF.Exp)),
                     reads=[B_T[ti]], writes=[B_PT[pi]])
            for kt in range(2):
                bk, Bk = next_bank()
                P.op("pe", (lambda e, bk=bk, kt=kt: e.matmul(bk[:, :], lhsT=KcT[:, c, kt * 128:(kt + 1) * 128],
                                                               rhs=QTz[:, pp, hh, 0:512], start=True, stop=True)),
                     reads=[B_KcT, B_QT[pp]], writes=[Bk])
                pi = next_pt()
                pis.append((pi, B_PT[pi].gen))
                P.op("act", (lambda e, bk=bk, pi=pi: e.activation(out=PT[:, pi, :], in_=bk[:, :], func=AF.Exp)),
                     reads=[Bk], writes=[B_PT[pi]])
            bk1, Bk1 = next_bank()
            for idx, a in enumerate([2, 3, 4, 5, 6]):
                na = 128 if a < 6 else 64
                P.op("pe", (lambda e, idx=idx, a=a, na=na: e.matmul(bk1[0:na, idx * 16:(idx + 1) * 16],
                                                                     lhsT=KT[:, pp, a * 128:a * 128 + na],
                                                                     rhs=QTz[:, pp, hh, 512:528], start=True, stop=True)),
                     reads=[B_KT[pp], B_QT[pp]], writes=[Bk1])
            for kt in range(2):
                P.op("pe", (lambda e, kt=kt: e.matmul(bk1[:, (5 + kt) * 16:(6 + kt) * 16], lhsT=KcT[:, c, kt * 128:(kt + 1) * 128],
                                                       rhs=QTz[:, pp, hh, 512:528], start=True, stop=True)),
                     reads=[B_KcT, B_QT[pp]], writes=[Bk1])
            b1b = b1[:, h * 5:(h + 1) * 5].rearrange("p (a o) -> p a o", o=1).broadcast_to([128, 5, 16])
            P.op("dve", (lambda e: e.tensor_tensor(out=t80[:, hh, :].rearrange("p (a q) -> p a q", q=16),
                                                     in0=bk1[:, 0:80].rearrange("p (a q) -> p a q", q=16), in1=b1b, op=ALU.add)),
                 reads=[Bk1, B_b1], writes=[B_t80[hh]])
            P.op("act", lambda e: e.activation(out=PT1[:, hh, 0:5, :], in_=t80[:, hh, :].rearrange("p (a q) -> p a q", q=16), func=AF.Exp),
                 reads=[B_t80[hh]], writes=[B_PT1[hh]])
            P.op("act", lambda e: e.activation(out=PT1[:, hh, 5:7, :], in_=bk1[:, 80:112].rearrange("p (a q) -> p a q", q=16), func=AF.Exp),
                 reads=[Bk1], writes=[B_PT1[hh]])
            return pis

        def sample_C(c, hh, pp, pis):
            h = 2 * c + hh
            bkpv, Bkpv = next_bank()
            for m in range(4):
                lst = [("l", a) for a in A_OF_M[m]] + [("c", 0), ("c", 1)]
                for i, (kind, a) in enumerate(lst):
                    if kind == "l":
                        pi, gen = pis[a]
                        P.op("pe", (lambda e, m=m, a=a, i=i, L=len(lst), pi=pi: e.matmul(
                            bkpv[:, m * 96:m * 96 + 66], lhsT=PT[:, pi, m * 128:(m + 1) * 128], rhs=Vaug[:, pp, a, hh, :],
                            start=(i == 0), stop=(i == L - 1))), reads=[(B_PT[pi], gen), B_Vaug[pp]], writes=[Bkpv])
                    else:
                        pi, gen = pis[6 + a]
                        P.op("pe", (lambda e, m=m, a=a, i=i, L=len(lst), pi=pi: e.matmul(
                            bkpv[:, m * 96:m * 96 + 66], lhsT=PT[:, pi, m * 128:(m + 1) * 128], rhs=VcA[:, a, h, :],
                            start=(i == 0), stop=(i == L - 1))), reads=[(B_PT[pi], gen), B_VcA], writes=[Bkpv])
            lst = [("l", 2), ("l", 3), ("l", 4), ("l", 5), ("l", 6), ("c", 0), ("c", 1)]
            for i, (kind, a) in enumerate(lst):
                if kind == "l":
                    na = 128 if a < 6 else 64
                    P.op("pe", (lambda e, a=a, i=i, na=na: e.matmul(
                        bkpv[0:16, 4 * 96:4 * 96 + 66], lhsT=PT1[0:na, hh, a - 2, :], rhs=Vaug[0:na, pp, a, hh, :],
                        start=(i == 0), stop=(i == 6))), reads=[B_PT1[hh], B_Vaug[pp]], writes=[Bkpv])
                else:
                    P.op("pe", (lambda e, a=a, i=i: e.matmul(
                        bkpv[0:16, 4 * 96:4 * 96 + 66], lhsT=PT1[:, hh, 5 + a, :], rhs=VcA[:, a, h, :],
                        start=(i == 0), stop=(i == 6))), reads=[B_PT1[hh], B_VcA], writes=[Bkpv])
            finish_pv(bkpv, Bkpv, 5, hh, pp)

        def phase_attn(tiles, prompt):
            slots = list(pre_qkv)
            del pre_qkv[:]
            for kc in range(len(slots), 8):
                slots.append(load_full(wqkv_d[kc * 128:(kc + 1) * 128, :]))
            for ti, (T, slot) in enumerate(tiles):
                if slot in attn_normed:
                    continue
                norm(T, slot, 8, 0, 1, tmpT, B_HH[slot])
                norm_apply(T, slot, 1, 8, 0, tmpT, B_HH[slot], T.col0)
            del attn_normed[:]
            pairs = [(ti, T, slot, c) for ti, (T, slot) in enumerate(tiles) for c in range(8)]
            n = len(pairs)

            def A_qk(i):
                ti, T, slot, c = pairs[i]
                qkv_qk(T, slot, slots, c, prompt, i % 2)

            def A_v(i):
                ti, T, slot, c = pairs[i]
                qkv_v(T, slot, slots, c, prompt, slot, i % 2)
            def fence():
                allkv = B_kv[0] + B_kv[1]
                P.op("dve", lambda e: e.memset(rc[:, 0, 7:8], 0.0), writes=allkv + B_stage + [B_rc[0]])

            def C2(i):
                ti, T, slot, c = pairs[i]
                blocks = [(m, 128, m * 128) for m in range(4)]
                if not prompt:
                    blocks = blocks + [(4, 16, 512)]
                otok_to_actT(T, slot, c, blocks, i % 2)
            if prompt:
                fence()
                A_qk(0)
                A_v(0)
                res = {0: prompt_B(pairs[0][3], 0)}
                if n > 1:
                    A_qk(1)
                    A_v(1)
                for i in range(n):
                    ti, T, slot, c = pairs[i]
                    prompt_C(T, slot, c, i % 2, res.pop(i))
                    if i + 1 < n:
                        res[i + 1] = prompt_B(pairs[i + 1][3], (i + 1) % 2)
                    if i + 2 < n:
                        A_qk(i + 2)
                        A_v(i + 2)
                    C2(i)
                fence()
            else:
                A_qk(0)
                A_v(0)
                for i in range(n):
                    ti, T, slot, c = pairs[i]
                    pp = i % 2
                    pis0 = sample_B(c, 0, pp)
                    ada_step(2)
                    if i > 0:
                        C2(i - 1)
                    if i + 1 < n:
                        A_qk(i + 1)
                    sample_C(c, 0, pp, pis0)
                    pis1 = sample_B(c, 1, pp)
                    ada_step(2)
                    if i + 1 < n:
                        A_v(i + 1)
                    sample_C(c, 1, pp, pis1)
                C2(n - 1)

        def final_out(T, slot, dram, ntok):
            norm(T, slot, 0, 0, 0, None, None)
            c0 = T.col0
            nblk = ntok // 128
            for b in range(nblk):
                si = b % 2
                for g in range(2):
                    bk, Bk = next_bank()
                    for cc in range(4):
                        c = g * 4 + cc
                        ei = 4 + (cc % 2)
                        col = c0 + b * 128
                        P.op("dve", (lambda e, c=c, ei=ei, col=col, b=b: e.scalar_tensor_tensor(
                            out=E[:, ei, 0:128], in0=xT[:, c, col:col + 128], scalar=pvc("gfin", c),
                            in1=rstd[:, b * 128:(b + 1) * 128], op0=ALU.mult, op1=ALU.mult)),
                            reads=[B_X[slot][c], B_rstd, B_pv], writes=[B_E[ei]])
                        P.op("pe", (lambda e, bk=bk, cc=cc, ei=ei: e.transpose(bk[:, cc * 128:(cc + 1) * 128], E[:, ei, 0:128], identf[:])),
                             reads=[B_E[ei], B_const], writes=[Bk])
                    P.op("act", (lambda e, bk=bk, si=si, g=g: e.copy(out=stage[:, si, g * 512:(g + 1) * 512], in_=bk[:, :])),
                         reads=[Bk], writes=[B_stage[si]])
                P.op("sp", (lambda e, si=si, b=b: e.dma_start(out=dram[b * 128:(b + 1) * 128, :], in_=stage[:, si, :])),
                     reads=[B_stage[si]], out_dma=True)

        def dbg_dump(ntok):
            for c in range(8):
                P.op("sp", (lambda e, c=c: e.dma_start(out=dbg_d[:, c, 0:ntok], in_=xT[:, c, 0:ntok])),
                     reads=[B_X[0][c], B_X[1][c]], out_dma=True)

        TS = Tile(0, [(0, 512), (512, 320)], [(0, 832)], 1)
        TS2 = Tile(0, [(0, 512), (512, 16)], [(0, 528)], 1)
        TS3 = Tile(0, [(0, 512)], [(0, 512)], 1)
        TP = [Tile(0, [(0, 512)], [(0, 256), (256, 512)], 0), Tile(528, [(0, 512)], [(0, 256), (256, 512)], 0)]
        tl = [(TP[1], 1), (TP[0], 0)]

        def fence_p1():
            P.op("dve", lambda e: e.memset(rc[:, 1, 7:8], 0.0), reads=B_X[0] + B_A[0] + B_HH[0],
                 writes=B_X[1] + B_A[1] + B_HH[1] + [B_rc[1]])
        p1_extras = [(0.5, fence_p1)] + [(3.0 + 4.0 * b, (lambda b=b: load_block(TP[1], 1, xp_d[512:1024, :], 512, b))) for b in range(4)]
        steps = [
            ("S_load", lambda: [load_block(TS, 0, xs_d, 832, b, start_stg()[b % 5]) for b in range(7)]),
            ("ada0", ada_startup),
            ("S_sconv", lambda: phase_sconv([(TS, 0)], ada0_extras)),
            ("S_proj0", lambda: phase_proj([(TS, 0)], wout_d, 0)),
            ("S_l0", lambda: phase_ffn([(TS, 0)], 0, ctx_extras)),
            ("S_ctx", lambda: None),
            ("S_attn", lambda: phase_attn([(TS, 0)], False)),
            ("S_proj1", lambda: phase_proj([(TS2, 0)], wo_d, 1)),
            ("S_l1", lambda: phase_ffn([(TS2, 0)], 1, p1_extras)),
            ("S_fin", lambda: final_out(TS3, 0, ys_d, 512)),
            ("P_load", lambda: None),
            ("P_sconv", lambda: phase_sconv(tl, [(0.5 + 1.0 * b, (lambda b=b: load_block(TP[0], 0, xp_d[0:512, :], 512, b))) for b in range(4)])),
            ("P_proj0", lambda: phase_proj(tl, wout_d, 0)),
            ("P_l0", lambda: phase_ffn(tl, 0)),
            ("P_attn", lambda: phase_attn(tl, True)),
            ("P_proj1", lambda: phase_proj(tl, wo_d, 1)),
            ("P_l1", lambda: phase_ffn(tl, 1)),
            ("P_fin", lambda: (final_out(TP[0], 0, yp_d[0:512, :], 512), final_out(TP[1], 1, yp_d[512:1024, :], 512))),
        ]
        def start_stg():
            Efl = E[:, :, :].rearrange("p a b -> p (a b)")
            return [(stage[:, 0, :], [B_stage[0]]), (stage[:, 1, :], [B_stage[1]]),
                    (Efl[:, 0:1024], [B_E[0], B_E[1]]), (Efl[:, 1664:2688], [B_E[2], B_E[3]]),
                    (Efl[:, 3328:4352], [B_E[4], B_E[5]])]

        for name, fn in steps:
            if name == "P_load":
                assert ada_state["done"] == 96
                state["nb"] = NB + 1
            fn()
            if DEBUG == name:
                dbg_dump(1024)
                break
        P.emit()
    return nc


def _fm(vec):
    return np.ascontiguousarray(np.asarray(vec, np.float32).reshape(-1, 128).T)


def _bias_tables(rpb, parity):
    rpb = np.asarray(rpb, np.float32)

    def table(krow_l, kcol_l, qrow_l, qcol_l):
        if parity:
            krow, kcol, qrow, qcol = 15 - krow_l, 63 - kcol_l, 15 - qrow_l, 63 - qcol_l
        else:
            krow, kcol, qrow, qcol = krow_l, kcol_l, qrow_l, qcol_l
        start = np.clip(qrow - 4, 0, 8)
        cs = np.clip(qcol - 8, 0, 48)
        valid = (krow >= start) & (krow < start + 8) & (kcol >= cs) & (kcol < cs + 16) & (krow >= 0) & (krow <= 15)
        dr = np.clip(krow - qrow + 7, 0, 14)
        dc = np.clip(kcol - qcol, -15, 15) + 15
        valid, dr, dc = np.broadcast_arrays(valid, dr, dc)
        out = rpb[:, dr, dc]
        return np.where(valid[None], out, np.float32(NEG)).astype(np.float32)

    k = np.arange(128)[:, None, None]
    a = np.arange(6)[None, :, None]
    q = np.arange(512)[None, None, :]
    ktok = a * 128 + k
    b0 = table(ktok // 64, ktok % 64, q // 64, q % 64)
    b0 = np.ascontiguousarray(b0.reshape(16, 128, 3072))
    a1 = np.arange(2, 7)[None, :, None]
    q1 = 512 + np.arange(16)[None, None, :]
    ktok1 = a1 * 128 + k
    b1 = table(ktok1 // 64, ktok1 % 64, q1 // 64, q1 % 64)
    b1 = np.ascontiguousarray(np.transpose(b1[:, :, :, 0], (1, 0, 2)).reshape(128, 16 * 5))
    return b0, b1


_NC_CACHE = {}


def kernel(x_prompt, x_sample, cache_k_ctx, cache_v_ctx, c, c_ctx, ada_w, ada_b, norm_mix_g, norm_ffn_g,
           sc_w_in, sc_conv_w, sc_conv_b, sc_w_out, na_w_qkv, na_rpb, na_w_o, ffn_w_up, ffn_conv_w,
           ffn_conv_b, ffn_w_down, final_g):
    f = lambda a: np.ascontiguousarray(np.asarray(a, np.float32))
    x_prompt, x_sample = f(x_prompt), f(x_sample)
    key = DEBUG
    if key not in _NC_CACHE:
        _NC_CACHE[key] = build_nc()
    nc = _NC_CACHE[key]
    tabs = [_bias_tables(na_rpb[0], p) for p in range(2)]
    shared = {"ada_w": f(ada_w), "sc_w_in": f(sc_w_in[0]), "sc_w_out": f(sc_w_out[0]), "na_w_qkv": f(na_w_qkv[0]),
              "na_w_o": f(na_w_o[0]), "ffn_w_up": f(ffn_w_up), "ffn_w_down": f(ffn_w_down)}
    in_maps = []
    for core in range(8):
        s, par = core // 2, core % 2
        xs_full = x_sample[s]
        xs_loc = xs_full[::-1][:832] if par else xs_full[:832]
        pvv = np.zeros((128, PV_N), np.float32)

        def put(name, arr):
            arr = _fm(arr)
            pvv[:, PV_OFF[name]:PV_OFF[name] + arr.shape[1]] = arr
        for l in range(2):
            put(f"ada_b{l}", ada_b[l])
            put(f"gmix{l}", norm_mix_g[l])
            put(f"gffn{l}", norm_ffn_g[l])
            fw = np.asarray(ffn_conv_w[l], np.float32)
            put(f"fcw{l}_0", fw.reshape(-1))
            put(f"fcw{l}_1", (fw[::-1] if par else fw).reshape(-1))
            put(f"fcb{l}", ffn_conv_b[l])
        sw = np.asarray(sc_conv_w[0], np.float32)
        put("scw_0", sw.reshape(-1))
        put("scw_1", (sw[::-1] if par else sw).reshape(-1))
        put("scb", sc_conv_b[0])
        put("gfin", final_g)
        cvv = np.stack([_fm(c_ctx), _fm(c[s])], axis=2).reshape(128, 16)
        m = {"xp": np.ascontiguousarray(x_prompt[4 * core:4 * core + 4].reshape(1024, D)),
             "xs": np.ascontiguousarray(xs_loc), "kc": f(cache_k_ctx[s, 0]), "vc": f(cache_v_ctx[s, 0]),
             "cvec": np.ascontiguousarray(cvv, np.float32), "pvec": pvv, "bias0": tabs[par][0], "bias1": tabs[par][1]}
        m.update(shared)
        in_maps.append(m)
    res = run_bass_kernel_spmd(nc, in_maps, core_ids=list(range(8)))
    rs = res.results
    y_prompt = np.empty((32, 256, D), np.float32)
    y_sample = np.empty((4, 1024, D), np.float32)
    sk = np.empty((32, 1, H, 256, 64), np.float32)
    sv = np.empty((32, 1, H, 256, 64), np.float32)
    for core in range(8):
        s, par = core // 2, core % 2
        r = rs[core]
        y_prompt[4 * core:4 * core + 4] = r["yp"].reshape(4, 256, D)
        if par:
            y_sample[s, 512:] = r["ys"][::-1]
        else:
            y_sample[s, :512] = r["ys"]
        sk[4 * core:4 * core + 4, 0] = r["sk"]
        sv[4 * core:4 * core + 4, 0] = r["sv"]
    if DEBUG:
        kernel.dbg = [rs[i]["dbg"] for i in range(8)]
    return (y_prompt, y_sample, sk, sv)
```
